# Optimizing a Trainium2 kernel written in Bass

```python
import math
import jax, jax.numpy as jnp
from jax import lax
import numpy as np

D_MODEL = 1024
BATCH = 32
SEQ = 2048
DEPTH = 1

HEAD_DIM = 64
MOBA_HEADS = 8
SB_HEADS = 8
MOBA_WIDTH = MOBA_HEADS * HEAD_DIM
SB_WIDTH = SB_HEADS * HEAD_DIM
N_BRANCHES = 2
GATE_WIDTH = N_BRANCHES * D_MODEL
IN_WIDTH = 3 * MOBA_WIDTH + 3 * SB_WIDTH + GATE_WIDTH
MOBA_BLOCK = 256
MOBA_TOPK = 3
Q_BLOCK = 128
D_FF = 4 * D_MODEL
ROPE_THETA = 10000.0
RMS_EPS = 1e-6
NEG = -1e30

kernel_name = "hybrid_moba_stickbreaking_gated_block"


def _rmsnorm(x, g):
    x32 = x.astype(jnp.float32)
    ms = jnp.mean(x32 * x32, axis=-1, keepdims=True)
    return (x32 * lax.rsqrt(ms + RMS_EPS) * g.astype(jnp.float32)).astype(x.dtype)


def _rope_tables(seq):
    half = HEAD_DIM // 2
    inv_freq = ROPE_THETA ** (-jnp.arange(half, dtype=jnp.float32) * 2.0 / HEAD_DIM)
    ang = jnp.arange(seq, dtype=jnp.float32)[:, None] * inv_freq[None, :]
    ang = jnp.concatenate([ang, ang], axis=-1)
    return jnp.cos(ang), jnp.sin(ang)


def _rope(t, cos, sin):
    t32 = t.astype(jnp.float32)
    half = HEAD_DIM // 2
    rot = jnp.concatenate([-t32[..., half:], t32[..., :half]], axis=-1)
    return (t32 * cos + rot * sin).astype(t.dtype)


def _to_heads(t, n_heads):
    b, s, _ = t.shape
    return t.reshape(b, s, n_heads, HEAD_DIM).transpose(0, 2, 1, 3)


def _from_heads(t):
    b, h, s, d = t.shape
    return t.transpose(0, 2, 1, 3).reshape(b, s, h * d)


def _moba_attention(q, k, v):
    b, h, s, d = q.shape
    nb = -(-s // MOBA_BLOCK)
    pad = nb * MOBA_BLOCK - s
    kp = jnp.pad(k, ((0, 0), (0, 0), (0, pad), (0, 0)))
    vp = jnp.pad(v, ((0, 0), (0, 0), (0, pad), (0, 0)))
    kblk = kp.reshape(b, h, nb, MOBA_BLOCK, d)
    vblk = vp.reshape(b, h, nb, MOBA_BLOCK, d)
    kmean = jnp.mean(kblk.astype(jnp.float32), axis=3)
    topk = min(MOBA_TOPK, nb)
    n_chunks = s // Q_BLOCK
    scale = 1.0 / math.sqrt(d)
    h_idx = jnp.arange(h)[:, None, None]

    def one_seq(args):
        qs, kbs, vbs, kms = args

        def one_chunk(c):
            start = c * Q_BLOCK
            qc = lax.dynamic_slice_in_dim(qs, start, Q_BLOCK, axis=1).astype(jnp.float32)
            qpos = start + jnp.arange(Q_BLOCK)
            own = start // MOBA_BLOCK
            gate = jnp.einsum('hqd,hnd->hqn', qc, kms)
            past = jnp.arange(nb) < own
            gate = jnp.where(past[None, None, :], gate, NEG)
            _, gidx = lax.top_k(gate, topk)
            valid = gidx < own
            ksel = kbs[h_idx, gidx]
            vsel = vbs[h_idx, gidx]
            s_sel = jnp.einsum('hqd,hqtkd->hqtk', qc, ksel.astype(jnp.float32)) * scale
            s_sel = jnp.where(valid[..., None], s_sel, NEG).reshape(h, Q_BLOCK, topk * MOBA_BLOCK)
            kown = lax.dynamic_index_in_dim(kbs, own, axis=1, keepdims=False).astype(jnp.float32)
            vown = lax.dynamic_index_in_dim(vbs, own, axis=1, keepdims=False).astype(jnp.float32)
            s_own = jnp.einsum('hqd,hkd->hqk', qc, kown) * scale
            kpos = own * MOBA_BLOCK + jnp.arange(MOBA_BLOCK)
            s_own = jnp.where(kpos[None, None, :] <= qpos[None, :, None], s_own, NEG)
            p = jax.nn.softmax(jnp.concatenate([s_sel, s_own], axis=-1), axis=-1)
            p_sel = p[..., :topk * MOBA_BLOCK].reshape(h, Q_BLOCK, topk, MOBA_BLOCK)
            p_own = p[..., topk * MOBA_BLOCK:]
            o = (jnp.einsum('hqtk,hqtkd->hqd', p_sel, vsel.astype(jnp.float32))
                 + jnp.einsum('hqk,hkd->hqd', p_own, vown))
            return o.astype(q.dtype)

        out = lax.map(one_chunk, jnp.arange(n_chunks))
        return out.transpose(1, 0, 2, 3).reshape(h, s, d)

    return lax.map(one_seq, (q, kblk, vblk, kmean))


def _stick_breaking_attention(q, k, v):
    b, h, s, d = q.shape
    scale = 1.0 / math.sqrt(d)
    outs = []
    for c in range(s // Q_BLOCK):
        start, end = c * Q_BLOCK, (c + 1) * Q_BLOCK
        qc = q[:, :, start:end].astype(jnp.float32)
        kc = k[:, :, :end].astype(jnp.float32)
        vc = v[:, :, :end].astype(jnp.float32)
        z = jnp.einsum('bhqd,bhkd->bhqk', qc, kc) * scale
        qpos = start + jnp.arange(Q_BLOCK)
        kpos = jnp.arange(end)
        mask = kpos[None, :] < qpos[:, None]
        log_1mb = jnp.where(mask, jax.nn.log_sigmoid(-z), 0.0)
        suffix = lax.cumsum(log_1mb, axis=3, reverse=True) - log_1mb
        a = jnp.where(mask, jnp.exp(jax.nn.log_sigmoid(z) + suffix), 0.0)
        outs.append(jnp.einsum('bhqk,bhkd->bhqd', a, vc).astype(q.dtype))
    return jnp.concatenate(outs, axis=2)


def setup_inputs(seed: int = 0) -> dict:
    key = jax.random.key(seed)
    ks = jax.random.split(key, 12)
    f32 = jnp.float32
    def nrm(k, shape, fan_in):
        return jax.random.normal(k, shape, f32) * (fan_in ** -0.5)
    def gain(k):
        return jnp.ones((DEPTH, D_MODEL), f32) + 0.05 * jax.random.normal(k, (DEPTH, D_MODEL), f32)
    return {
        "x": jax.random.normal(ks[0], (BATCH, SEQ, D_MODEL), f32),
        "g_pre_mix": gain(ks[1]),
        "w_in": nrm(ks[2], (DEPTH, D_MODEL, IN_WIDTH), D_MODEL),
        "b_gate": 0.02 * jax.random.normal(ks[3], (DEPTH, GATE_WIDTH), f32),
        "w_up_moba": nrm(ks[4], (DEPTH, MOBA_WIDTH, D_MODEL), MOBA_WIDTH),
        "w_up_sb": nrm(ks[5], (DEPTH, SB_WIDTH, D_MODEL), SB_WIDTH),
        "w_out": nrm(ks[6], (DEPTH, D_MODEL, D_MODEL), D_MODEL),
        "g_post_mix": gain(ks[7]),
        "g_pre_mlp": gain(ks[8]),
        "w_mlp_in": nrm(ks[9], (DEPTH, D_MODEL, D_FF), D_MODEL),
        "w_mlp_out": nrm(ks[10], (DEPTH, D_FF, D_MODEL), D_FF),
        "g_post_mlp": gain(ks[11]),
    }


def reference(x, g_pre_mix, w_in, b_gate, w_up_moba, w_up_sb, w_out, g_post_mix,
              g_pre_mlp, w_mlp_in, w_mlp_out, g_post_mlp):
    b, s, _ = x.shape
    cos, sin = _rope_tables(s)
    cos, sin = cos.astype(x.dtype), sin.astype(x.dtype)
    split_at = list(np.cumsum([MOBA_WIDTH] * 3 + [SB_WIDTH] * 3))
    for l in range(DEPTH):
        hn = _rmsnorm(x, g_pre_mix[l])
        proj = jnp.einsum('bsd,de->bse', hn, w_in[l])
        qa, ka, va, qb, kb, vb, gates = jnp.split(proj, split_at, axis=-1)
        qa = _rope(_to_heads(qa, MOBA_HEADS), cos, sin)
        ka = _rope(_to_heads(ka, MOBA_HEADS), cos, sin)
        va = _to_heads(va, MOBA_HEADS)
        qb, kb, vb = (_to_heads(t, SB_HEADS) for t in (qb, kb, vb))
        y_moba = _from_heads(_moba_attention(qa, ka, va))
        y_sb = _from_heads(_stick_breaking_attention(qb, kb, vb))
        y_moba = jnp.einsum('bse,ed->bsd', y_moba, w_up_moba[l])
        y_sb = jnp.einsum('bse,ed->bsd', y_sb, w_up_sb[l])
        g = jax.nn.sigmoid((gates + b_gate[l]).astype(jnp.float32)).astype(x.dtype)
        g_moba, g_sb = g[..., :D_MODEL], g[..., D_MODEL:]
        mixed = g_moba * y_moba + g_sb * y_sb
        mix_out = jnp.einsum('bsd,de->bse', mixed, w_out[l])
        x = x + _rmsnorm(mix_out, g_post_mix[l])
        hn = _rmsnorm(x, g_pre_mlp[l])
        u = jnp.einsum('bsd,df->bsf', hn, w_mlp_in[l])
        u = jnp.square(jax.nn.relu(u))
        m = jnp.einsum('bsf,fd->bsd', u, w_mlp_out[l])
        x = x + _rmsnorm(m, g_post_mlp[l])
    return x
```

```python
import numpy as np
import concourse.bass as bass
import concourse.mybir as mybir
from concourse.bass_utils import run_bass_kernel_spmd

F32 = mybir.dt.float32
BF16 = mybir.dt.bfloat16
AF = mybir.ActivationFunctionType
ALU = mybir.AluOpType
AX = mybir.AxisListType

SEQ = 2048
DM = 1024
NCORES = 8
EPS = 1e-6
BIG = 30000.0

C_COS, C_SIN, C_PB, C_C1 = 0, 2048, 4096, 4224
C_B16 = 4352
B_ID, B_RA, B_RB, B_NTI, B_NEG1, B_NMS, B_NMM = 0, 128, 192, 256, 384, 512, 2560
NB16 = 4608
NCF = C_B16 + NB16


class Op:
    __slots__ = ("eng", "fn", "deps", "need_sig", "sig_val", "dma_sem", "dma_val", "epoch")

    def __init__(self, eng, fn, epoch):
        self.eng = eng
        self.fn = fn
        self.deps = []
        self.need_sig = False
        self.sig_val = None
        self.dma_sem = None
        self.dma_val = None
        self.epoch = epoch


class Sched:
    ENGS = ("pe", "act", "dve", "pool", "sp")

    def __init__(self, nc):
        self.nc = nc
        self.order = {e: [] for e in self.ENGS}
        self.lastw = {}
        self.readers = {}
        self.sems = [{e: nc.alloc_semaphore("prog%d_%s" % (k, e)) for e in self.ENGS} for k in range(3)]
        self.dma_cnt = {}
        self.dma_last = {}
        self.epoch = 0

    def op(self, eng, fn, reads=(), writes=(), dma_sem=None):
        o = Op(eng, fn, self.epoch)
        deps = {}
        for k in reads:
            w = self.lastw.get(k)
            if w is not None:
                deps[id(w)] = w
        for k in writes:
            w = self.lastw.get(k)
            if w is not None:
                deps[id(w)] = w
            for r in self.readers.get(k, ()):
                deps[id(r)] = r
        for d in deps.values():
            if d.dma_sem is None and d.eng == "pe" and eng == "pe":
                continue
            o.deps.append(d)
            if d.dma_sem is None:
                d.need_sig = True
        for k in writes:
            self.lastw[k] = o
            self.readers[k] = []
        for k in reads:
            self.readers.setdefault(k, []).append(o)
        if dma_sem is not None:
            c = self.dma_cnt.get(id(dma_sem), 0) + 16
            self.dma_cnt[id(dma_sem)] = c
            o.dma_sem = dma_sem
            o.dma_val = c
            self.dma_last[id(dma_sem)] = o
        self.order[eng].append(o)
        return o

    def barrier(self):
        lasts = []
        for e in self.ENGS:
            for o in reversed(self.order[e]):
                if o.epoch != self.epoch:
                    break
                if o.dma_sem is None and o.fn is not None and o.fn != "clear":
                    lasts.append(o)
                    break
        dmas = list(self.dma_last.values())
        for e in self.ENGS:
            b = Op(e, None, self.epoch)
            for d in lasts:
                if not (d.eng == e and e == "pe"):
                    b.deps.append(d)
                    d.need_sig = True
            b.deps.extend(dmas)
            self.order[e].append(b)
        self.lastw = {}
        self.readers = {}
        self.epoch += 1
        for e in self.ENGS:
            self.order[e].append(Op(e, "clear", self.epoch))

    def emit(self, final_waits=()):
        nc = self.nc
        for e in self.ENGS:
            c = 0
            ep = 0
            for o in self.order[e]:
                if o.epoch != ep:
                    ep = o.epoch
                    c = 0
                if o.dma_sem is None and o.need_sig:
                    c += 1
                    o.sig_val = c
            assert c < 30000
        handles = {"pe": "tensor", "act": "scalar", "dve": "vector", "pool": "gpsimd", "sp": "sync"}
        allsems = self.sems
        with nc.Block() as block:
            for e in self.ENGS:
                ops = self.order[e]

                def body(engine, ops=ops, e=e):
                    waited = {}
                    for o in ops:
                        if o.fn == "clear":
                            engine.sem_clear(allsems[(o.epoch + 1) % 3][e])
                            continue
                        for d in o.deps:
                            if d.dma_sem is not None:
                                s, v = d.dma_sem, d.dma_val
                            else:
                                s, v = allsems[d.epoch % 3][d.eng], d.sig_val
                            key = (id(s), d.epoch if d.dma_sem is None else -1)
                            if waited.get(key, 0) >= v:
                                continue
                            waited[key] = v
                            engine.wait_ge(s, v)
                        if o.fn is None:
                            continue
                        ins = o.fn(engine)
                        if o.dma_sem is not None:
                            ins.then_inc(o.dma_sem, 16)
                        elif o.need_sig:
                            ins.then_inc(allsems[o.epoch % 3][e], 1)
                    if e == "sp":
                        for o in final_waits:
                            engine.wait_ge(o.dma_sem, o.dma_val)

                getattr(block, handles[e])(body)


class Arena:
    def __init__(self, nc, name, nbytes):
        self.t = nc.alloc_sbuf_tensor(name, [128, nbytes // 2], BF16)
        self.nbytes = nbytes
        self.off = 0
        self.peak = 0

    def alloc(self, shape, dt):
        n = 1
        for s in shape:
            n *= s
        nb = n * (4 if dt == F32 else 2)
        nb = (nb + 63) // 64 * 64
        off = self.off
        assert off + nb <= self.nbytes, ("arena overflow", off, nb, self.nbytes)
        self.off += nb
        self.peak = max(self.peak, self.off)
        ap = self.t[:, off // 2:(off + nb) // 2]
        if dt == F32:
            ap = ap.bitcast(F32)
        ap = ap[:, 0:n]
        if len(shape) == 2:
            return ap.rearrange("p (a b) -> p a b", a=shape[0])
        if len(shape) == 3:
            return ap.rearrange("p (a b c) -> p a b c", a=shape[0], b=shape[1])
        return ap


class StopBuild(Exception):
    pass


def build(nseq=4, dbg=None, upto=None):
    nc = bass.Bass("TRN2", target_bir_lowering=False)

    def din(name, shape):
        return nc.dram_tensor(name, shape, F32, kind="ExternalInput").ap()

    x = din("x", [nseq, SEQ, DM])
    w_in = din("w_in", [1024, 5120])
    w_upm = din("w_up_moba", [512, 1024])
    w_ups = din("w_up_sb", [512, 1024])
    w_out = din("w_out", [1024, 1024])
    w_mi = din("w_mlp_in", [1024, 4096])
    w_mo = din("w_mlp_out", [4096, 1024])
    gvec = din("gvec", [128, 32])
    gbc = din("gbc", [2, 1024])
    cf = din("cf", [128, NCF])
    erow = din("erow", [8, SEQ])
    out = nc.dram_tensor("out", [nseq, SEQ, DM], F32, kind="ExternalOutput").ap()
    dbg_out = {}
    if dbg:
        for nm, shp in dbg.items():
            dbg_out[nm] = nc.dram_tensor("dbg_" + nm, shp, BF16, kind="ExternalOutput").ap()

    WinL = nc.dram_tensor("WinL", [40, 128, 8, 128], BF16).ap()
    WmiL = nc.dram_tensor("WmiL", [32, 128, 8, 128], BF16).ap()
    WupmL = nc.dram_tensor("WupmL", [8, 128, 4, 128], BF16).ap()
    WupsL = nc.dram_tensor("WupsL", [8, 128, 4, 128], BF16).ap()
    WoutR = nc.dram_tensor("WoutR", [8, 128, 1024], BF16).ap()
    WmoR = nc.dram_tensor("WmoR", [32, 128, 1024], BF16).ap()

    S = Sched(nc)
    nsem = [0]

    def newsem(name):
        nsem[0] += 1
        return nc.alloc_semaphore("%s_%d" % (name, nsem[0]))

    constb = nc.alloc_sbuf_tensor("constb", [128, NB16], BF16)
    ident = constb[:, B_ID:B_ID + 128]
    RA = constb[:, B_RA:B_RA + 64]
    RB = constb[:, B_RB:B_RB + 64]
    IA = constb[:, B_ID:B_ID + 64]
    IB = constb[:, B_ID + 64:B_ID + 128]
    NTI = constb[:, B_NTI:B_NTI + 128]
    NEG1 = constb[:, B_NEG1:B_NEG1 + 128]
    NMS = [constb[:, B_NMS + 512 * j:B_NMS + 512 * (j + 1)] for j in range(4)]
    NMM = [constb[:, B_NMM + 512 * j:B_NMM + 512 * (j + 1)] for j in range(4)]
    constf = nc.alloc_sbuf_tensor("constf", [128, 256], F32)
    PB = constf[:, 0:128].rearrange("p (a b) -> p a b", a=16)
    C1 = constf[:, 128:256].rearrange("p (a b) -> p a b", a=16)
    gv = nc.alloc_sbuf_tensor("gv", [128, 32], F32)
    gB = nc.alloc_sbuf_tensor("gB", [128, 2, 1024], F32)
    yTm = nc.alloc_sbuf_tensor("yTm", [128, 4, SEQ], BF16)
    yTs = nc.alloc_sbuf_tensor("yTs", [128, 4, SEQ], BF16)
    KaT = [nc.alloc_sbuf_tensor("KaT%d" % i, [128, SEQ], BF16) for i in range(2)]
    NW = 8
    wring = nc.alloc_sbuf_tensor("wring", [128, NW, 1024], BF16)
    wsem = [newsem("w") for _ in range(NW)]
    stat = nc.alloc_sbuf_tensor("stat", [128, 64], F32)
    junk = nc.alloc_sbuf_tensor("junk", [128, 1024], BF16)
    psb = [nc.alloc_psum_tensor("psb%d" % i, [128, 512], F32) for i in range(8)]

    arena = Arena(nc, "arena", 128 * 1024)

    wcnt = [0]

    def wload(src, nel):
        i = wcnt[0] % NW
        wcnt[0] += 1
        dst = wring[:, i, 0:nel]
        S.op("sp", lambda e, dst=dst, src=src: e.dma_start(out=dst, in_=src), writes=[("wr", i)], dma_sem=wsem[i])
        return dst, ("wr", i)

    m0 = arena.off
    NPS = 6
    stf = [arena.alloc([2048], F32) for _ in range(NPS)]
    stb = [arena.alloc([2048], BF16) for _ in range(NPS)]
    sem_in = [newsem("pin") for _ in range(NPS)]
    sem_out = [newsem("pout") for _ in range(NPS)]
    sem_c = newsem("const")
    pcnt = [0]

    def cast_op(eng, dst, src, reads, writes):
        if eng == "act":
            S.op("act", lambda e: e.activation(out=dst, in_=src, func=AF.Copy), reads=reads, writes=writes)
        else:
            S.op(eng, lambda e: e.tensor_copy(out=dst, in_=src), reads=reads, writes=writes)

    def prep_block(src, n, dst=None, dst_sb=None):
        k = pcnt[0] % NPS
        eng = ("dve", "act", "dve", "act", "pool")[pcnt[0] % 5]
        pcnt[0] += 1
        S.op("sp", lambda e: e.dma_start(out=stf[k][:, 0:n], in_=src), writes=[("stf", k)], dma_sem=sem_in[k])
        if dst_sb is not None:
            cast_op(eng, dst_sb, stf[k][:, 0:n], [("stf", k)], [("constb",)])
            return
        cast_op(eng, stb[k][:, 0:n], stf[k][:, 0:n], [("stf", k)], [("stb", k)])
        srcv = stb[k][:, 0:n]
        if len(dst.shape) == 3:
            srcv = srcv.rearrange("p (g j) -> p g j", j=128)
        S.op("sp", lambda e: e.dma_start(out=dst, in_=srcv), reads=[("stb", k)], dma_sem=sem_out[k])

    S.op("sp", lambda e: e.dma_start(out=constf[:, :], in_=cf[:, C_PB:C_PB + 256]), writes=[("constf",)], dma_sem=sem_c)
    S.op("sp", lambda e: e.dma_start(out=gv[:, :], in_=gvec), writes=[("gv",)], dma_sem=sem_c)
    S.op("sp", lambda e: e.dma_start(out=gB[:, :, :], in_=bass.AP(tensor=gbc.tensor, offset=0, ap=[[0, 128], [1024, 2], [1, 1024]])), writes=[("gB",)], dma_sem=sem_c)
    for n0 in range(0, NB16, 2048):
        n = min(2048, NB16 - n0)
        prep_block(cf[:, C_B16 + n0:C_B16 + n0 + n], n, dst_sb=constb[:, n0:n0 + n])
    S.op("sp", lambda e: e.dma_start(out=stf[0][64:72, 0:SEQ], in_=erow), reads=[("constb",)], writes=[("stf", 0)], dma_sem=sem_in[0])
    for i in range(2):
        S.op("dve", lambda e, i=i: e.tensor_copy(out=KaT[i][64:72, :], in_=stf[0][64:72, 0:SEQ]),
             reads=[("stf", 0)], writes=[("KaE", i)])

    def blocks_lhsT(src, K, n_lo, n_hi, dstL, bs):
        for c in range(K // 128):
            for n0 in range(n_lo, n_hi, bs):
                n = min(bs, n_hi - n0)
                g0 = n0 // 128
                d = dstL[g0:g0 + n // 128, :, c, :].rearrange("g p j -> p g j")
                yield (src[c * 128:(c + 1) * 128, n0:n0 + n], n, d)

    def blocks_rhs(src, K, N, dstR):
        for c in range(K // 128):
            yield (src[c * 128:(c + 1) * 128, 0:N], N, dstR[c, :, :])

    for (src_, n_, d_) in blocks_lhsT(w_in, 1024, 0, 3072, WinL, 2048):
        prep_block(src_, n_, dst=d_)
    bg_blocks = []
    bg_blocks += list(blocks_lhsT(w_in, 1024, 3072, 5120, WinL, 1024))
    bg_blocks += list(blocks_lhsT(w_upm, 512, 0, 1024, WupmL, 1024))
    bg_blocks += list(blocks_lhsT(w_ups, 512, 0, 1024, WupsL, 1024))
    bg_blocks += list(blocks_rhs(w_out, 1024, 1024, WoutR))
    bg_blocks += list(blocks_lhsT(w_mi, 1024, 0, 4096, WmiL, 1024))
    bg_blocks += list(blocks_rhs(w_mo, 4096, 1024, WmoR))
    S.barrier()
    arena.off = m0
    if upto == 'prep':
        S.emit(final_waits=[])
        return nc

    mA = arena.off
    hnT = arena.alloc([8, SEQ], BF16)
    tab = arena.alloc([2, SEQ], F32)
    QaT = [arena.alloc([SEQ], BF16) for _ in range(2)]
    QbT = arena.alloc([SEQ], BF16)
    KbT = arena.alloc([SEQ], BF16)
    Vp = arena.alloc([16, 256], BF16)
    Vp4 = Vp.rearrange("p t (h c) -> p t h c", h=2)
    xt = [arena.alloc([1024], F32) for _ in range(2)]
    xn = [arena.alloc([1024], BF16) for _ in range(2)]
    cq = [arena.alloc([512], BF16) for _ in range(2)]
    sq = [arena.alloc([512], BF16) for _ in range(2)]
    E1 = [arena.alloc([512], F32) for _ in range(2)]
    NL = 5
    Lp = [arena.alloc([512], BF16) for _ in range(NL)]
    AT = [arena.alloc([512], BF16) for _ in range(3)]
    Rb = [arena.alloc([512], BF16) for _ in range(3)]
    recb = [arena.alloc([512], F32) for _ in range(2)]
    gm = arena.alloc([16, 8], F32)
    mx = arena.alloc([16, 8], F32)
    thr = arena.alloc([16], F32)
    selm = arena.alloc([16, 8], F32)
    MbPs = [arena.alloc([16, 72], BF16) for _ in range(2)]
    kms = arena.alloc([16], F32)
    kmT = arena.alloc([16], BF16)
    endA = arena.off
    arena.off = mA
    xch = arena.alloc([4, 1024], F32)
    hnTc = arena.alloc([8, 512], BF16)
    xn2 = [arena.alloc([1024], BF16) for _ in range(2)]
    gT = arena.alloc([16, 512], BF16)
    mixT = arena.alloc([8, 512], BF16)
    t1 = [arena.alloc([512], F32) for _ in range(2)]
    t2 = [arena.alloc([512], F32) for _ in range(2)]
    tmpn = [arena.alloc([1024], F32) for _ in range(2)]
    u2T = arena.alloc([32, 512], BF16)
    rl = [arena.alloc([512], BF16) for _ in range(3)]
    ost = [arena.alloc([1024], F32) for _ in range(2)]
    endC = arena.off
    arena.off = max(endA, endC)
    bstf = [arena.alloc([1024], F32) for _ in range(2)]
    bstb = [arena.alloc([1024], BF16) for _ in range(2)]
    bsem_in = [newsem("bin") for _ in range(2)]
    bsem_out = [newsem("bout") for _ in range(2)]
    bgc = [0]

    def bg_step(eng):
        if not bg_blocks:
            return
        src, n, dst = bg_blocks.pop(0)
        k = bgc[0] % 2
        bgc[0] += 1
        S.op("sp", lambda e: e.dma_start(out=bstf[k][:, 0:n], in_=src), writes=[("bstf", k)], dma_sem=bsem_in[k])
        cast_op(eng, bstb[k][:, 0:n], bstf[k][:, 0:n], [("bstf", k)], [("bstb", k)])
        srcv = bstb[k][:, 0:n]
        if len(dst.shape) == 3:
            srcv = srcv.rearrange("p (g j) -> p g j", j=128)
        S.op("sp", lambda e: e.dma_start(out=dst, in_=srcv), reads=[("bstb", k)], dma_sem=bsem_out[k])

    xsem = [newsem("x") for _ in range(4)]
    osem = [newsem("o") for _ in range(2)]
    tsem = newsem("tab")
    dsem = newsem("dbg")
    stc = [0]

    gpre = gv[:, 0:8]
    gpre2 = gv[:, 8:16]
    bgate = gv[:, 16:32]

    def ps_bf(b):
        return psb[b][:, :].bitcast(BF16).rearrange("p (c t) -> p c t", c=8)

    tb = [0]

    def norm_transpose(src, src_keys, dst, dst_keys, g, xnbuf, xnkey, banks, split=False):
        k = stc[0] % 16
        stc[0] += 1
        c0 = 4 * k
        sk = ("st", k)
        S.op("act", lambda e: e.activation(out=junk[:, :], in_=src, func=AF.Square, accum_out=stat[:, c0:c0 + 1]),
             reads=src_keys, writes=[sk])
        S.op("act", lambda e: e.activation(out=stat[:, c0 + 1:c0 + 2], in_=stat[:, c0:c0 + 1], func=AF.Sqrt,
                                           scale=1.0 / DM, bias=EPS), reads=[sk], writes=[sk])
        S.op("dve", lambda e: e.reciprocal(out=stat[:, c0 + 2:c0 + 3], in_=stat[:, c0 + 1:c0 + 2]), reads=[sk], writes=[sk])
        j = tb[0] % 2
        tb[0] += 1
        S.op("dve", lambda e: e.tensor_scalar(out=xnbuf[j], in0=src, scalar1=stat[:, c0 + 2:c0 + 3], scalar2=None,
                                              op0=ALU.mult), reads=src_keys + [sk], writes=[(xnkey, j)])
        bank = banks[j]
        pv = ps_bf(bank)

        def part2():
            def tr(e):
                for dc in range(8):
                    ins = e.transpose(out=pv[:, dc, :], in_=xnbuf[j][:, dc * 128:(dc + 1) * 128], identity=ident)
                return ins
            S.op("pe", tr, reads=[(xnkey, j)], writes=[("ps", bank)])
            S.op("dve", lambda e: e.tensor_tensor(out=dst, in0=pv, in1=g.unsqueeze(2).to_broadcast([128, 8, 128]), op=ALU.mult),
                 reads=[("ps", bank)], writes=dst_keys)
        if split:
            return part2
        part2()

    def proj_fm(wt, wkey, rhs_fn, rkeys, bank):
        def fn(e):
            for dc in range(8):
                ins = e.matmul(psb[bank][:, :], lhsT=wt[:, dc * 128:(dc + 1) * 128], rhs=rhs_fn(dc), start=(dc == 0), stop=(dc == 7))
            return ins
        S.op("pe", fn, reads=[wkey] + rkeys, writes=[("ps", bank)])

    def hn_chunk(tc):
        return (lambda dc: hnT[:, dc, tc * 512:(tc + 1) * 512]), [("hnT", 4 * tc + i) for i in range(4)]

    final_ops = []
    dbg_ops = []

    def dbg_dump(name, src_ap, keys, dst_ap=None):
        if dbg and name in dbg_out:
            d = dbg_out[name] if dst_ap is None else dst_ap
            o = S.op("sp", lambda e: e.dma_start(out=d, in_=src_ap), reads=keys, dma_sem=dsem)
            dbg_ops.append(o)

    def seq_body(s):
        S.op("sp", lambda e: e.dma_start(out=tab[:, :, :], in_=cf[:, 0:4096].rearrange("p (a b) -> p a b", a=2)),
             writes=[("tab",)], dma_sem=tsem)
        S.op("pool", lambda e: e.memset(Vp[:, :, :], 1.0), writes=[("Vp", t) for t in range(16)])
        for hl_ in range(2):
            S.op("pool", lambda e, hl_=hl_: e.memset(MbPs[hl_][:, :, :], 0.0), writes=[("MbP", hl_)])
        for tt in range(16):
            k = tt % 2
            S.op("sp", lambda e, k=k, tt=tt: e.dma_start(out=xt[k], in_=x[s, tt * 128:(tt + 1) * 128, :]),
                 writes=[("xt", k)], dma_sem=xsem[k])
            norm_transpose(xt[k], [("xt", k)], hnT[:, :, tt * 128:(tt + 1) * 128], [("hnT", tt)], gpre, xn, "xn", (0, 1))
        if s == 0:
            dbg_dump("hnT", hnT[:, :, :], [("hnT", t) for t in range(16)])
        if upto == 'A':
            raise StopBuild()

        def moba_pair(hp):
            stages = []
            for which in ("k", "q"):
                g = (4 + hp) if which == "k" else hp
                wt, wkey = wload(WinL[g].rearrange("p c j -> p (c j)"), 1024)
                for tc in range(4):
                    stages.append((which, tc, wt, wkey))

            def st_proj(i):
                which, tc, wt, wkey = stages[i]
                rf, rk = hn_chunk(tc)
                proj_fm(wt, wkey, rf, rk, i % 2)

            def st_rest(i):
                which, tc, wt, wkey = stages[i]
                pb = i % 2
                S.op("dve", lambda e: e.tensor_tensor(out=cq[pb], in0=psb[pb][:, :], in1=tab[:, 0, tc * 512:(tc + 1) * 512], op=ALU.mult),
                     reads=[("ps", pb), ("tab",)], writes=[("cq", pb)])
                S.op("dve", lambda e: e.tensor_tensor(out=sq[pb], in0=psb[pb][:, :], in1=tab[:, 1, tc * 512:(tc + 1) * 512], op=ALU.mult),
                     reads=[("ps", pb), ("tab",)], writes=[("sq", pb)])
                for hl in range(2):
                    rb = 2 + 2 * hl + pb
                    I_, R_ = (IA, RA) if hl == 0 else (IB, RB)

                    def rope(e, rb=rb, I_=I_, R_=R_):
                        e.matmul(psb[rb][0:64, :], lhsT=I_, rhs=cq[pb], start=True, stop=False)
                        return e.matmul(psb[rb][0:64, :], lhsT=R_, rhs=sq[pb], start=False, stop=True)
                    S.op("pe", rope, reads=[("cq", pb), ("sq", pb)], writes=[("ps", rb)])
                    if which == "k":
                        def kev(e, rb=rb, hl=hl):
                            for b2 in range(2):
                                ins = e.activation(out=KaT[hl][0:64, tc * 512 + b2 * 256:tc * 512 + (b2 + 1) * 256],
                                                   in_=psb[rb][0:64, b2 * 256:(b2 + 1) * 256], func=AF.Copy,
                                                   accum_out=kms[0:64, hl * 8 + 2 * tc + b2:hl * 8 + 2 * tc + b2 + 1])
                            return ins
                        S.op("act", kev, reads=[("ps", rb)], writes=[("KaT", hl, tc), ("kms", hl, tc)])
                    else:
                        S.op("act", lambda e, rb=rb, hl=hl: e.activation(out=QaT[hl][0:64, tc * 512:(tc + 1) * 512], in_=psb[rb][0:64, :], func=AF.Copy, scale=0.125),
                             reads=[("ps", rb)], writes=[("QaT", hl, tc)])
                if which == "k" and tc == 3:
                    S.op("dve", lambda e: e.tensor_copy(out=kmT[0:64, :], in_=kms[0:64, :]),
                         reads=[("kms", hl, t) for hl in range(2) for t in range(4)], writes=[("kmT",)])

            st_proj(0)
            for i in range(8):
                if i + 1 < 8:
                    st_proj(i + 1)
                st_rest(i)
            if upto == 'kq':
                raise StopBuild()
            wt, wkey = wload(WinL[8 + hp].rearrange("p c j -> p (c j)"), 1024)
            for tg in range(4):
                pb = tg % 2

                def vproj(e, tg=tg, pb=pb, wt=wt):
                    for i in range(4):
                        tt = 4 * tg + i
                        for dc in range(8):
                            ins = e.matmul(psb[pb][:, i * 128:(i + 1) * 128], lhsT=hnT[:, dc, tt * 128:(tt + 1) * 128],
                                           rhs=wt[:, dc * 128:(dc + 1) * 128], start=(dc == 0), stop=(dc == 7))
                    return ins
                S.op("pe", vproj, reads=[wkey] + [("hnT", 4 * tg + i) for i in range(4)], writes=[("ps", pb)])
                S.op("act", lambda e, tg=tg, pb=pb: e.activation(out=Vp4[:, 4 * tg:4 * tg + 4, :, 0:64], in_=psb[pb][:, :].rearrange("p (t h c) -> p t h c", t=4, h=2), func=AF.Copy),
                     reads=[("ps", pb)], writes=[("Vp", 4 * tg + i) for i in range(4)])
            if upto == 'v':
                raise StopBuild()
            for hl in (range(1) if upto == 'gate0' else range(2)):
                def gate(e, hl=hl):
                    for qt in range(16):
                        ins = e.matmul(psb[6][:, qt * 8:(qt + 1) * 8], lhsT=QaT[hl][0:64, qt * 128:(qt + 1) * 128],
                                       rhs=kmT[0:64, hl * 8:(hl + 1) * 8], start=True, stop=True)
                    return ins
                S.op("pe", gate, reads=[("QaT", hl, tc) for tc in range(4)] + [("kmT",)], writes=[("ps", 6)])
                S.op("dve", lambda e, hl=hl: e.tensor_tensor(out=gm, in0=psb[6][:, 0:128].rearrange("p (a b) -> p a b", a=16), in1=PB, op=ALU.add),
                     reads=[("ps", 6)], writes=[("gm",)])

                def top8(e):
                    for qt in range(16):
                        ins = e.max(out=mx[:, qt, :], in_=gm[:, qt, :])
                    return ins
                S.op("dve", top8, reads=[("gm",)], writes=[("mx",)])
                S.op("dve", lambda e: e.tensor_scalar(out=thr, in0=mx[:, :, 2], scalar1=-1e29, scalar2=None, op0=ALU.max),
                     reads=[("mx",)], writes=[("thr",)])
                S.op("dve", lambda e: e.tensor_tensor(out=selm, in0=gm, in1=thr.unsqueeze(2).to_broadcast([128, 16, 8]), op=ALU.is_ge),
                     reads=[("gm",), ("thr",)], writes=[("selm",)])
                S.op("dve", lambda e, hl=hl: e.tensor_tensor(out=MbPs[0][:, :, 64:72], in0=selm, in1=C1, op=ALU.add),
                     reads=[("selm",)], writes=[("MbP", 0)])
                for tc in range(4):
                    def mtr(e, tc=tc, hl=hl):
                        for i in range(4):
                            qt = 4 * tc + i
                            ins = e.matmul(psb[7][0:72, i * 128:(i + 1) * 128], lhsT=MbPs[0][:, qt, :], rhs=ident, start=True, stop=True)
                        return ins
                    S.op("pe", mtr, reads=[("MbP", 0)], writes=[("ps", 7)])
                    S.op("dve", lambda e, tc=tc, hl=hl: e.tensor_copy(out=QaT[hl][64:72, tc * 512:(tc + 1) * 512], in_=psb[7][64:72, :]),
                         reads=[("ps", 7)], writes=[("QaM", hl, tc)])
            if s == 0 and hp == 0:
                dbg_dump("QaT0", QaT[0][:, :], [("QaT", 0, tc) for tc in range(4)] + [("QaM", 0, tc) for tc in range(4)])
                dbg_dump("KaT0", KaT[0][:, :], [("KaT", 0, tc) for tc in range(4)] + [("KaE", 0)])
            if upto in ('gate', 'gate0'):
                raise StopBuild()
            tiles = []
            chain = 0
            for c in range(4):
                for hl in range(2):
                    nk = 4 * c + 4
                    for kt in range(nk):
                        tiles.append((c, hl, kt, nk, chain))
                    chain += 1
            N = len(tiles)

            def rng(c, kt):
                j = kt - 4 * c
                return (128 * j if j > 0 else 0), j

            def m_qk(i):
                c, hl, kt, nk, ch = tiles[i]
                sbk = i % 3
                lo, j = rng(c, kt)

                def fn(e):
                    ins = e.matmul(psb[sbk][:, lo:512], lhsT=KaT[hl][0:72, kt * 128:(kt + 1) * 128], rhs=QaT[hl][0:72, c * 512 + lo:(c + 1) * 512],
                                   start=True, stop=(j < 0))
                    if j >= 0:
                        ins = e.matmul(psb[sbk][:, lo:lo + 128], lhsT=ident, rhs=NMM[j][:, lo:lo + 128], start=False, stop=True)
                    return ins
                S.op("pe", fn, reads=[("KaT", hl, kt // 4), ("KaE", hl), ("QaT", hl, c), ("QaM", hl, c)], writes=[("ps", sbk)])

            def m_exp(i):
                c, hl, kt, nk, ch = tiles[i]
                sbk = i % 3
                r = i % 3
                lo, j = rng(c, kt)
                S.op("act", lambda e: e.activation(out=AT[r][:, lo:512], in_=psb[sbk][:, lo:512], func=AF.Exp), reads=[("ps", sbk)], writes=[("AT", r)])

            def m_pv(i):
                c, hl, kt, nk, ch = tiles[i]
                r = i % 3
                ob = 4 + ch % 2
                lo, j = rng(c, kt)
                S.op("pe", lambda e: e.matmul(psb[ob][:, lo:512], lhsT=Vp4[:, kt, hl, :], rhs=AT[r][:, lo:512], start=(kt == 0), stop=(kt == nk - 1)),
                     reads=[("AT", r), ("Vp", kt)], writes=[("ps", ob)])
                if kt == nk - 1:
                    rr = ch % 2
                    po = 64 * hl
                    S.op("dve", lambda e: e.reciprocal(out=recb[rr][64:128, :], in_=psb[ob][64:128, :]), reads=[("ps", ob)], writes=[("rec", rr)])
                    S.op("dve", lambda e: e.tensor_tensor(out=yTm[po:po + 64, hp, c * 512:(c + 1) * 512], in0=psb[ob][0:64, :],
                                                          in1=recb[rr][64:128, :], op=ALU.mult),
                         reads=[("ps", ob), ("rec", rr)], writes=[("yTm", hp, hl, c)])

            m_qk(0)
            for st_ in range(N + 1):
                if s == 0 and st_ % 6 == 3:
                    bg_step("pool")
                if st_ + 1 < N:
                    m_qk(st_ + 1)
                if st_ < N:
                    m_exp(st_)
                if 0 <= st_ - 1 < N:
                    m_pv(st_ - 1)
        for hp_ in range(4):
            moba_pair(hp_)
            if upto == 'moba1':
                break
        if s == 0:
            dbg_dump("yTm", yTm[:, :, :], [("yTm", hp, hl, c) for hp in range(4) for hl in range(2) for c in range(4)])

        def sb_pair(hp):
            for which in ("q", "k"):
                g = (12 + hp) if which == "q" else (16 + hp)
                wt, wkey = wload(WinL[g].rearrange("p c j -> p (c j)"), 1024)
                dstT = QbT if which == "q" else KbT
                for tc in range(4):
                    pb = 6 + tc % 2
                    rf, rk = hn_chunk(tc)
                    proj_fm(wt, wkey, rf, rk, pb)
                    sc = 0.125 if which == "q" else 1.0
                    S.op("dve", lambda e, pb=pb, tc=tc, dstT=dstT, sc=sc: e.tensor_scalar(out=dstT[:, tc * 512:(tc + 1) * 512], in0=psb[pb][:, :], scalar1=sc, scalar2=None, op0=ALU.mult),
                         reads=[("ps", pb)], writes=[(which + "bT", tc)])
            wt, wkey = wload(WinL[20 + hp].rearrange("p c j -> p (c j)"), 1024)
            for tg in range(4):
                pb = 6 + tg % 2

                def vproj2(e, tg=tg, pb=pb, wt=wt):
                    for i in range(4):
                        tt = 4 * tg + i
                        for dc in range(8):
                            ins = e.matmul(psb[pb][:, i * 128:(i + 1) * 128], lhsT=hnT[:, dc, tt * 128:(tt + 1) * 128],
                                           rhs=wt[:, dc * 128:(dc + 1) * 128], start=(dc == 0), stop=(dc == 7))
                    return ins
                S.op("pe", vproj2, reads=[wkey] + [("hnT", 4 * tg + i) for i in range(4)], writes=[("ps", pb)])
                S.op("dve", lambda e, tg=tg, pb=pb: e.tensor_copy(out=Vp4[:, 4 * tg:4 * tg + 4, :, 0:64], in_=psb[pb][:, :].rearrange("p (t h c) -> p t h c", t=4, h=2)),
                     reads=[("ps", pb)], writes=[("Vp", 4 * tg + i) for i in range(4)])
            tiles = []
            chain = 0
            for c in range(4):
                for hl in range(2):
                    nk = 4 * c + 4
                    for n in range(nk):
                        tiles.append((c, hl, nk - 1 - n, n, nk, chain))
                    chain += 1
            N = len(tiles)
            rsrc = {}

            def s_qk(i):
                c, hl, kt, n, nk, ch = tiles[i]
                zb = i % 4
                po = 64 * hl
                diag = kt >= 4 * c

                def fn(e):
                    ins = e.matmul(psb[zb][:, :], lhsT=KbT[po:po + 64, kt * 128:(kt + 1) * 128], rhs=QbT[po:po + 64, c * 512:(c + 1) * 512],
                                   start=True, stop=not diag)
                    if diag:
                        ins = e.matmul(psb[zb][:, :], lhsT=ident, rhs=NMS[kt - 4 * c], start=False, stop=True)
                    return ins
                S.op("pe", fn, reads=[("kbT", kt // 4), ("qbT", c)], writes=[("ps", zb)])

            def s_e1(i):
                zb = i % 4
                r = i % 2
                S.op("act", lambda e: e.activation(out=E1[r], in_=psb[zb][:, :], func=AF.Exp), reads=[("ps", zb)], writes=[("E1", r)])

            def s_lp(i):
                r = i % 2
                l = i % NL
                S.op("act", lambda e: e.activation(out=Lp[l], in_=E1[r], func=AF.Ln, bias=1.0), reads=[("E1", r)], writes=[("Lp", l)])

            def s_rupd(i):
                c, hl, kt, n, nk, ch = tiles[i]
                if n + 1 >= nk:
                    return
                if n == 0:
                    rsrc[i + 1] = (Lp[i % NL], ("Lp", i % NL))
                    return
                prev, pkey = rsrc[i]
                rn = i % 3
                S.op("pool", lambda e: e.tensor_tensor(out=Rb[rn], in0=prev, in1=Lp[i % NL], op=ALU.add),
                     reads=[pkey, ("Lp", i % NL)], writes=[("R", rn)])
                rsrc[i + 1] = (Rb[rn], ("R", rn))

            def s_trir(i):
                c, hl, kt, n, nk, ch = tiles[i]
                zb = i % 4
                l = i % NL
                reads = [("Lp", l)]
                rs = None
                if n >= 1:
                    rs, rkey = rsrc[i]
                    reads.append(rkey)

                def fn(e):
                    ins = e.matmul(psb[zb][:, :], lhsT=NTI, rhs=Lp[l], start=False, stop=(n == 0), skip_group_check=True)
                    if n >= 1:
                        ins = e.matmul(psb[zb][:, :], lhsT=NEG1, rhs=rs, start=False, stop=True, skip_group_check=True)
                    return ins
                S.op("pe", fn, reads=reads, writes=[("ps", zb)])

            def s_at(i):
                zb = i % 4
                r = i % 3
                S.op("act", lambda e: e.activation(out=AT[r], in_=psb[zb][:, :], func=AF.Exp), reads=[("ps", zb)], writes=[("AT", r)])

            def s_av(i):
                c, hl, kt, n, nk, ch = tiles[i]
                r = i % 3
                ob = 4 + ch % 2
                S.op("pe", lambda e: e.matmul(psb[ob][:, :], lhsT=Vp4[:, kt, hl, :], rhs=AT[r], start=(n == 0), stop=(n == nk - 1)),
                     reads=[("AT", r), ("Vp", kt)], writes=[("ps", ob)])
                if n == nk - 1:
                    po = 64 * hl
                    S.op("dve", lambda e: e.tensor_copy(out=yTs[po:po + 64, hp, c * 512:(c + 1) * 512], in_=psb[ob][0:64, :]),
                         reads=[("ps", ob)], writes=[("yTs", hp, hl, c)])

            s_qk(0)
            for st_ in range(N + 3):
                if s == 0 and st_ % 6 == 3:
                    bg_step("dve")
                if st_ + 1 < N:
                    s_qk(st_ + 1)
                if st_ < N:
                    s_e1(st_)
                if 0 <= st_ - 2 < N:
                    s_at(st_ - 2)
                if st_ < N:
                    s_lp(st_)
                    s_rupd(st_)
                if 0 <= st_ - 1 < N:
                    s_trir(st_ - 1)
                if 0 <= st_ - 3 < N:
                    s_av(st_ - 3)
        if upto in ('moba', 'moba1'):
            dbg_dump("yTm", yTm[:, :, :], [])
            raise StopBuild()
        for hp_ in range(4):
            sb_pair(hp_)
            if upto == 'sb1':
                break
        if s == 0:
            dbg_dump("yTs", yTs[:, :, :], [("yTs", hp, hl, c) for hp in range(4) for hl in range(2) for c in range(4)])

        if upto in ('sb', 'sb1'):
            raise StopBuild()
        while s == 0 and bg_blocks:
            bg_step("dve")
        S.barrier()
        hk = [("hnTc", i) for i in range(4)]

        def c_head(tc, tiles_):
            p2 = []
            for i in tiles_:
                tok0 = tc * 512 + i * 128
                S.op("sp", lambda e, i=i, tok0=tok0: e.dma_start(out=xch[:, i, :], in_=x[s, tok0:tok0 + 128, :]),
                     writes=[("xch", i)], dma_sem=xsem[i])
                p2.append(norm_transpose(xch[:, i, :], [("xch", i)], hnTc[:, :, i * 128:(i + 1) * 128], [("hnTc", i)], gpre, xn2, "xn2", (0, 1), split=True))
            return p2

        def c_gates(half):
            hkh = [("hnTc", 2 * half), ("hnTc", 2 * half + 1)]
            for j in range(16):
                wt, wkey = wload(WinL[24 + j].rearrange("p c j -> p (c j)"), 1024)
                pb = 2 + j % 2

                def fn(e, wt=wt, pb=pb):
                    for dc in range(8):
                        ins = e.matmul(psb[pb][:, 0:256], lhsT=wt[:, dc * 128:(dc + 1) * 128], rhs=hnTc[:, dc, half * 256:(half + 1) * 256],
                                       start=(dc == 0), stop=(dc == 7))
                    return ins
                S.op("pe", fn, reads=[wkey] + hkh, writes=[("ps", pb)])
                S.op("act", lambda e, j=j, pb=pb: e.activation(out=gT[:, j, half * 256:(half + 1) * 256], in_=psb[pb][:, 0:256], func=AF.Sigmoid, bias=bgate[:, j:j + 1]),
                     reads=[("ps", pb), ("gv",)], writes=[("gT", j, half)])

        def c_upmix(tc):
            for j in range(8):
                wm, wmk = wload(WupmL[j].rearrange("p c j -> p (c j)"), 512)
                ws, wsk = wload(WupsL[j].rearrange("p c j -> p (c j)"), 512)
                bm = 4 + j % 2
                bs = 6 + j % 2
                r = j % 2

                def up(e, wm=wm, ws=ws, bm=bm, bs=bs):
                    for p in range(4):
                        e.matmul(psb[bm][:, :], lhsT=wm[:, p * 128:(p + 1) * 128], rhs=yTm[:, p, tc * 512:(tc + 1) * 512], start=(p == 0), stop=(p == 3))
                    for p in range(4):
                        ins = e.matmul(psb[bs][:, :], lhsT=ws[:, p * 128:(p + 1) * 128], rhs=yTs[:, p, tc * 512:(tc + 1) * 512], start=(p == 0), stop=(p == 3))
                    return ins
                S.op("pe", up, reads=[wmk, wsk], writes=[("ps", bm), ("ps", bs)])
                S.op("dve", lambda e, j=j, bm=bm, r=r: e.tensor_tensor(out=t1[r], in0=psb[bm][:, :], in1=gT[:, j, :], op=ALU.mult),
                     reads=[("ps", bm), ("gT", j, 0), ("gT", j, 1)], writes=[("t1", r)])
                S.op("dve", lambda e, j=j, bs=bs, r=r: e.tensor_tensor(out=t2[r], in0=psb[bs][:, :], in1=gT[:, 8 + j, :], op=ALU.mult),
                     reads=[("ps", bs), ("gT", 8 + j, 0), ("gT", 8 + j, 1)], writes=[("t2", r)])
                S.op("pool", lambda e, j=j, r=r: e.tensor_tensor(out=mixT[:, j, :], in0=t1[r], in1=t2[r], op=ALU.add),
                     reads=[("t1", r), ("t2", r)], writes=[("mixT", j)])

        def post_norm(i, gsel, dst, dst_keys, res_keys):
            k = stc[0] % 16
            stc[0] += 1
            c0 = 4 * k
            sk = ("st", k)
            b0, b1 = 2 * i, 2 * i + 1

            def sqs(e):
                e.activation(out=junk[:, 0:512], in_=psb[b0][:, :], func=AF.Square, accum_out=stat[:, c0:c0 + 1])
                return e.activation(out=junk[:, 512:1024], in_=psb[b1][:, :], func=AF.Square, accum_out=stat[:, c0 + 1:c0 + 2])
            S.op("act", sqs, reads=[("ps", b0), ("ps", b1)], writes=[sk])
            S.op("dve", lambda e: e.tensor_tensor(out=stat[:, c0 + 2:c0 + 3], in0=stat[:, c0:c0 + 1], in1=stat[:, c0 + 1:c0 + 2], op=ALU.add),
                 reads=[sk], writes=[sk])
            S.op("act", lambda e: e.activation(out=stat[:, c0 + 3:c0 + 4], in_=stat[:, c0 + 2:c0 + 3], func=AF.Sqrt, scale=1.0 / DM, bias=EPS),
                 reads=[sk], writes=[sk])
            S.op("dve", lambda e: e.reciprocal(out=stat[:, c0:c0 + 1], in_=stat[:, c0 + 3:c0 + 4]), reads=[sk], writes=[sk])
            r = i % 2

            def nrm(e):
                e.scalar_tensor_tensor(out=tmpn[r][:, 0:512], in0=psb[b0][:, :], scalar=stat[:, c0:c0 + 1], in1=gB[:, gsel, 0:512],
                                       op0=ALU.mult, op1=ALU.mult)
                return e.scalar_tensor_tensor(out=tmpn[r][:, 512:1024], in0=psb[b1][:, :], scalar=stat[:, c0:c0 + 1], in1=gB[:, gsel, 512:1024],
                                              op0=ALU.mult, op1=ALU.mult)
            S.op("dve", nrm, reads=[("ps", b0), ("ps", b1), sk, ("gB",)], writes=[("tmpn", r)])
            S.op("pool", lambda e: e.tensor_tensor(out=dst, in0=xch[:, i, :], in1=tmpn[r], op=ALU.add),
                 reads=[("tmpn", r)] + res_keys, writes=dst_keys)

        def c_postmid(i):
            post_norm(i, 0, xch[:, i, :], [("xch", i)], [("xch", i)])

        def c_dnorm(i):
            return norm_transpose(xch[:, i, :], [("xch", i)], hnTc[:, :, i * 128:(i + 1) * 128], [("hnTc", i)], gpre2, xn2, "xn2", (0, 1), split=True)

        def c_mlpin(fs, mode):
            c_lo, c_n = ((0, 256), (256, 256), (0, 512))[mode]
            hs = ((0,), (1,), (0, 1))[mode]
            hkh = [("hnTc", 2 * h_ + k_) for h_ in hs for k_ in range(2)]
            for f in fs:
                wt, wkey = wload(WmiL[f].rearrange("p c j -> p (c j)"), 1024)
                pb = (f % 4) if mode == 0 else (2 + f % 4)

                def fn(e, wt=wt, pb=pb):
                    for dc in range(8):
                        ins = e.matmul(psb[pb][:, 0:c_n], lhsT=wt[:, dc * 128:(dc + 1) * 128], rhs=hnTc[:, dc, c_lo:c_lo + c_n],
                                       start=(dc == 0), stop=(dc == 7))
                    return ins
                S.op("pe", fn, reads=[wkey] + hkh, writes=[("ps", pb)])
                r = f % 3
                S.op("act", lambda e, pb=pb, r=r: e.activation(out=rl[r][:, 0:c_n], in_=psb[pb][:, 0:c_n], func=AF.Relu), reads=[("ps", pb)], writes=[("rl", r)])
                S.op("pool" if f % 2 == 0 else "dve", lambda e, f=f, r=r: e.tensor_tensor(out=u2T[:, f, c_lo:c_lo + c_n], in0=rl[r][:, 0:c_n], in1=rl[r][:, 0:c_n], op=ALU.mult),
                     reads=[("rl", r)], writes=[("u2T", f, h_) for h_ in hs])

        def c_acc(fs, tl, src_fn, skey_fn, wsrc, f_first, f_last):
            for f in fs:
                wo, wok = wload(wsrc[f], 1024)

                def acc(e, f=f, wo=wo):
                    for i in tl:
                        for h in range(2):
                            ins = e.matmul(psb[2 * i + h][:, :], lhsT=src_fn(f, i), rhs=wo[:, h * 512:(h + 1) * 512],
                                           start=(f == f_first[i]), stop=(f == f_last[i]))
                    return ins
                S.op("pe", acc, reads=[wok] + skey_fn(f, tl), writes=[("ps", 2 * i + h) for i in tl for h in range(2)])

        def c_postfin(tc, i):
            r = i % 2
            post_norm(i, 1, ost[r], [("ost", r)], [("xch", i)])
            tok0 = tc * 512 + i * 128
            o = S.op("sp", lambda e, r=r, tok0=tok0: e.dma_start(out=out[s, tok0:tok0 + 128, :], in_=ost[r]),
                     reads=[("ost", r)], dma_sem=osem[r])
            final_ops.append(o)

        def mix_src(j, i):
            return mixT[:, j, i * 128:(i + 1) * 128]

        def mix_keys(j, tl):
            return [("mixT", j)]

        def u2_src(f, i):
            return u2T[:, f, i * 128:(i + 1) * 128]

        def u2_keys(f, tl):
            return [("u2T", f, h_) for h_ in sorted(set(i // 2 for i in tl))]

        for p2 in c_head(0, (0, 1)):
            p2()
        nx = c_head(0, (2, 3))
        for tc in range(4):
            c_gates(0)
            for p2 in nx:
                p2()
            c_gates(1)
            c_upmix(tc)
            c_acc(range(0, 4), (0, 1), mix_src, mix_keys, WoutR, {0: 0, 1: 0}, {0: 7, 1: 7})
            c_acc(range(4, 8), (0, 1, 2, 3), mix_src, mix_keys, WoutR, {0: 0, 1: 0, 2: 4, 3: 4}, {0: 7, 1: 7, 2: 3, 3: 3})
            c_postmid(0)
            c_postmid(1)
            p2s = [c_dnorm(0), c_dnorm(1)]
            c_acc(range(0, 4), (2, 3), mix_src, mix_keys, WoutR, {2: 4, 3: 4}, {2: 3, 3: 3})
            for p2 in p2s:
                p2()
            c_postmid(2)
            c_postmid(3)
            nx = [c_dnorm(2), c_dnorm(3)]
            c_mlpin(range(0, 16), 0)
            for p2 in nx:
                p2()
            c_mlpin(range(16, 32), 2)
            c_mlpin(range(0, 16), 1)
            c_acc(range(0, 16), (0, 1), u2_src, u2_keys, WmoR, {0: 0, 1: 0}, {0: 31, 1: 31})
            c_acc(range(16, 32), (0, 1, 2, 3), u2_src, u2_keys, WmoR, {0: 0, 1: 0, 2: 16, 3: 16}, {0: 31, 1: 31, 2: 15, 3: 15})
            c_postfin(tc, 0)
            c_postfin(tc, 1)
            p2s = c_head(tc + 1, (0, 1)) if tc < 3 else []
            c_acc(range(0, 16), (2, 3), u2_src, u2_keys, WmoR, {2: 16, 3: 16}, {2: 15, 3: 15})
            for p2 in p2s:
                p2()
            c_postfin(tc, 2)
            c_postfin(tc, 3)
            if tc < 3:
                nx = c_head(tc + 1, (2, 3))
        S.barrier()

    try:
        for s_ in range(nseq):
            seq_body(s_)
    except StopBuild:
        S.barrier()
    S.emit(final_waits=final_ops[-2:] + dbg_ops)
    return nc


def _consts():
    cfa = np.zeros((128, NCF), np.float32)
    half = 32
    inv = (np.float32(10000.0) ** (-(np.arange(half, dtype=np.float32) * np.float32(2.0)) / np.float32(64))).astype(np.float32)
    ang = (np.arange(SEQ, dtype=np.float32)[:, None] * inv[None, :]).astype(np.float32)
    ang = np.concatenate([ang, ang], axis=-1)
    cos = np.cos(ang).astype(np.float32)
    sin = np.sin(ang).astype(np.float32)
    p = np.arange(128)
    cfa[:, C_COS:C_COS + SEQ] = cos[:, p % 64].T
    cfa[:, C_SIN:C_SIN + SEQ] = sin[:, p % 64].T
    qt = np.arange(16)[:, None]
    b = np.arange(8)[None, :]
    own = qt // 2
    pbm = np.where(b < own, 0.0, -1e30).astype(np.float32)
    c1m = (b == own).astype(np.float32) - 1.0
    cfa[:, C_PB:C_PB + 128] = pbm.reshape(1, 128)
    cfa[:, C_C1:C_C1 + 128] = c1m.reshape(1, 128)
    B = np.zeros((128, NB16), np.float32)
    B[:, B_ID:B_ID + 128] = np.eye(128)
    for m in range(64):
        for base, col in ((0, B_RA), (64, B_RB)):
            if m < 32:
                B[base + m + 32, col + m] = -1.0
            else:
                B[base + m - 32, col + m] = 1.0
    jj = np.arange(128)[:, None]
    kk = np.arange(128)[None, :]
    B[:, B_NTI:B_NTI + 128] = np.where(jj >= kk, -1.0, 0.0)
    B[:, B_NEG1:B_NEG1 + 128] = -1.0
    k = np.arange(128)[:, None]
    q = np.arange(512)[None, :]
    for j in range(4):
        i = q // 128
        qq = q % 128
        inv_s = (i < j) | ((i == j) & (k >= qq))
        inv_m = (i < j) | ((i == j) & (k > qq))
        B[:, B_NMS + 512 * j:B_NMS + 512 * (j + 1)] = np.where(inv_s, -BIG, 0.0)
        B[:, B_NMM + 512 * j:B_NMM + 512 * (j + 1)] = np.where(inv_m, -BIG, 0.0)
    cfa[:, C_B16:] = B
    er = np.zeros((8, SEQ), np.float32)
    for bb in range(8):
        er[bb, bb * 256:(bb + 1) * 256] = BIG
    return cfa, er


_NC_CACHE = {}


def kernel(x, g_pre_mix, w_in, b_gate, w_up_moba, w_up_sb, w_out, g_post_mix,
           g_pre_mlp, w_mlp_in, w_mlp_out, g_post_mlp):
    f = lambda a: np.ascontiguousarray(np.asarray(a, dtype=np.float32))
    x = f(x)
    nseq = x.shape[0] // NCORES
    if nseq not in _NC_CACHE:
        _NC_CACHE[nseq] = build(nseq)
    nc = _NC_CACHE[nseq]
    cfa, er = _consts()
    gvec = np.concatenate([f(g_pre_mix)[0].reshape(8, 128).T, f(g_pre_mlp)[0].reshape(8, 128).T,
                           f(b_gate)[0].reshape(16, 128).T], axis=1)
    gbc = np.stack([f(g_post_mix)[0], f(g_post_mlp)[0]], axis=0)
    shared = {
        "w_in": f(w_in)[0], "w_up_moba": f(w_up_moba)[0], "w_up_sb": f(w_up_sb)[0], "w_out": f(w_out)[0],
        "w_mlp_in": f(w_mlp_in)[0], "w_mlp_out": f(w_mlp_out)[0],
        "gvec": np.ascontiguousarray(gvec), "gbc": np.ascontiguousarray(gbc), "cf": cfa, "erow": er,
    }
    in_maps = []
    for c in range(NCORES):
        m = dict(shared)
        m["x"] = np.ascontiguousarray(x[c * nseq:(c + 1) * nseq])
        in_maps.append(m)
    res = run_bass_kernel_spmd(nc, in_maps, core_ids=list(range(NCORES)))
    return np.concatenate([r["out"] for r in res.results], axis=0)
```

```python
import numpy as np
import concourse.bass as bass
import concourse.mybir as mybir
from concourse.bass_utils import run_bass_kernel_spmd

F32 = mybir.dt.float32
BF16 = mybir.dt.bfloat16
AF = mybir.ActivationFunctionType
ALU = mybir.AluOpType
AX = mybir.AxisListType

SEQ = 2048
DM = 1024
NCORES = 8
EPS = 1e-6
BIG = 30000.0

C_COS, C_SIN, C_PB, C_C1 = 0, 2048, 4096, 4224
C_B16 = 4352
B_ID, B_RA, B_RB, B_NTI, B_NEG1, B_NMS, B_NMM = 0, 128, 192, 256, 384, 512, 2560
NB16 = 4608
NCF = C_B16 + NB16


class Op:
    __slots__ = ("eng", "fn", "deps", "need_sig", "sig_val", "dma_sem", "dma_val", "epoch")

    def __init__(self, eng, fn, epoch):
        self.eng = eng
        self.fn = fn
        self.deps = []
        self.need_sig = False
        self.sig_val = None
        self.dma_sem = None
        self.dma_val = None
        self.epoch = epoch


class Sched:
    ENGS = ("pe", "act", "dve", "pool", "sp")

    def __init__(self, nc):
        self.nc = nc
        self.order = {e: [] for e in self.ENGS}
        self.lastw = {}
        self.readers = {}
        self.sems = [{e: nc.alloc_semaphore("prog%d_%s" % (k, e)) for e in self.ENGS} for k in range(3)]
        self.dma_cnt = {}
        self.dma_last = {}
        self.epoch = 0

    def op(self, eng, fn, reads=(), writes=(), dma_sem=None):
        o = Op(eng, fn, self.epoch)
        deps = {}
        for k in reads:
            w = self.lastw.get(k)
            if w is not None:
                deps[id(w)] = w
        for k in writes:
            w = self.lastw.get(k)
            if w is not None:
                deps[id(w)] = w
            for r in self.readers.get(k, ()):
                deps[id(r)] = r
        for d in deps.values():
            if d.dma_sem is None and d.eng == "pe" and eng == "pe":
                continue
            o.deps.append(d)
            if d.dma_sem is None:
                d.need_sig = True
        for k in writes:
            self.lastw[k] = o
            self.readers[k] = []
        for k in reads:
            self.readers.setdefault(k, []).append(o)
        if dma_sem is not None:
            c = self.dma_cnt.get(id(dma_sem), 0) + 16
            self.dma_cnt[id(dma_sem)] = c
            o.dma_sem = dma_sem
            o.dma_val = c
            self.dma_last[id(dma_sem)] = o
        self.order[eng].append(o)
        return o

    def barrier(self):
        lasts = []
        for e in self.ENGS:
            for o in reversed(self.order[e]):
                if o.epoch != self.epoch:
                    break
                if o.dma_sem is None and o.fn is not None and o.fn != "clear":
                    lasts.append(o)
                    break
        dmas = list(self.dma_last.values())
        for e in self.ENGS:
            b = Op(e, None, self.epoch)
            for d in lasts:
                if not (d.eng == e and e == "pe"):
                    b.deps.append(d)
                    d.need_sig = True
            b.deps.extend(dmas)
            self.order[e].append(b)
        self.lastw = {}
        self.readers = {}
        self.epoch += 1
        for e in self.ENGS:
            self.order[e].append(Op(e, "clear", self.epoch))

    def emit(self, final_waits=()):
        nc = self.nc
        for e in self.ENGS:
            c = 0
            ep = 0
            for o in self.order[e]:
                if o.epoch != ep:
                    ep = o.epoch
                    c = 0
                if o.dma_sem is None and o.need_sig:
                    c += 1
                    o.sig_val = c
            assert c < 30000
        handles = {"pe": "tensor", "act": "scalar", "dve": "vector", "pool": "gpsimd", "sp": "sync"}
        allsems = self.sems
        with nc.Block() as block:
            for e in self.ENGS:
                ops = self.order[e]

                def body(engine, ops=ops, e=e):
                    waited = {}
                    for o in ops:
                        if o.fn == "clear":
                            engine.sem_clear(allsems[(o.epoch + 1) % 3][e])
                            continue
                        for d in o.deps:
                            if d.dma_sem is not None:
                                s, v = d.dma_sem, d.dma_val
                            else:
                                s, v = allsems[d.epoch % 3][d.eng], d.sig_val
                            key = (id(s), d.epoch if d.dma_sem is None else -1)
                            if waited.get(key, 0) >= v:
                                continue
                            waited[key] = v
                            engine.wait_ge(s, v)
                        if o.fn is None:
                            continue
                        ins = o.fn(engine)
                        if o.dma_sem is not None:
                            ins.then_inc(o.dma_sem, 16)
                        elif o.need_sig:
                            ins.then_inc(allsems[o.epoch % 3][e], 1)
                    if e == "sp":
                        for o in final_waits:
                            engine.wait_ge(o.dma_sem, o.dma_val)

                getattr(block, handles[e])(body)


class Arena:
    def __init__(self, nc, name, nbytes):
        self.t = nc.alloc_sbuf_tensor(name, [128, nbytes // 2], BF16)
        self.nbytes = nbytes
        self.off = 0
        self.peak = 0

    def alloc(self, shape, dt):
        n = 1
        for s in shape:
            n *= s
        nb = n * (4 if dt == F32 else 2)
        nb = (nb + 63) // 64 * 64
        off = self.off
        assert off + nb <= self.nbytes, ("arena overflow", off, nb, self.nbytes)
        self.off += nb
        self.peak = max(self.peak, self.off)
        ap = self.t[:, off // 2:(off + nb) // 2]
        if dt == F32:
            ap = ap.bitcast(F32)
        ap = ap[:, 0:n]
        if len(shape) == 2:
            return ap.rearrange("p (a b) -> p a b", a=shape[0])
        if len(shape) == 3:
            return ap.rearrange("p (a b c) -> p a b c", a=shape[0], b=shape[1])
        return ap


class StopBuild(Exception):
    pass


def build(nseq=4, dbg=None, upto=None):
    nc = bass.Bass("TRN2", target_bir_lowering=False)

    def din(name, shape):
        return nc.dram_tensor(name, shape, F32, kind="ExternalInput").ap()

    x = din("x", [nseq, SEQ, DM])
    w_in = din("w_in", [1024, 5120])
    w_upm = din("w_up_moba", [512, 1024])
    w_ups = din("w_up_sb", [512, 1024])
    w_out = din("w_out", [1024, 1024])
    w_mi = din("w_mlp_in", [1024, 4096])
    w_mo = din("w_mlp_out", [4096, 1024])
    gvec = din("gvec", [128, 32])
    gbc = din("gbc", [2, 1024])
    cf = din("cf", [128, NCF])
    erow = din("erow", [8, SEQ])
    out = nc.dram_tensor("out", [nseq, SEQ, DM], F32, kind="ExternalOutput").ap()
    dbg_out = {}
    if dbg:
        for nm, shp in dbg.items():
            dbg_out[nm] = nc.dram_tensor("dbg_" + nm, shp, BF16, kind="ExternalOutput").ap()

    WinL = nc.dram_tensor("WinL", [40, 128, 8, 128], BF16).ap()
    WmiL = nc.dram_tensor("WmiL", [32, 128, 8, 128], BF16).ap()
    WupmL = nc.dram_tensor("WupmL", [8, 128, 4, 128], BF16).ap()
    WupsL = nc.dram_tensor("WupsL", [8, 128, 4, 128], BF16).ap()
    WoutR = nc.dram_tensor("WoutR", [8, 128, 1024], BF16).ap()
    WmoR = nc.dram_tensor("WmoR", [32, 128, 1024], BF16).ap()

    S = Sched(nc)
    nsem = [0]

    def newsem(name):
        nsem[0] += 1
        return nc.alloc_semaphore("%s_%d" % (name, nsem[0]))

    constb = nc.alloc_sbuf_tensor("constb", [128, NB16], BF16)
    ident = constb[:, B_ID:B_ID + 128]
    RA = constb[:, B_RA:B_RA + 64]
    RB = constb[:, B_RB:B_RB + 64]
    IA = constb[:, B_ID:B_ID + 64]
    IB = constb[:, B_ID + 64:B_ID + 128]
    NTI = constb[:, B_NTI:B_NTI + 128]
    NEG1 = constb[:, B_NEG1:B_NEG1 + 128]
    NMS = [constb[:, B_NMS + 512 * j:B_NMS + 512 * (j + 1)] for j in range(4)]
    NMM = [constb[:, B_NMM + 512 * j:B_NMM + 512 * (j + 1)] for j in range(4)]
    constf = nc.alloc_sbuf_tensor("constf", [128, 256], F32)
    PB = constf[:, 0:128].rearrange("p (a b) -> p a b", a=16)
    C1 = constf[:, 128:256].rearrange("p (a b) -> p a b", a=16)
    gv = nc.alloc_sbuf_tensor("gv", [128, 32], F32)
    gB = nc.alloc_sbuf_tensor("gB", [128, 2, 1024], F32)
    yTm = nc.alloc_sbuf_tensor("yTm", [128, 4, SEQ], BF16)
    yTs = nc.alloc_sbuf_tensor("yTs", [128, 4, SEQ], BF16)
    KaT = [nc.alloc_sbuf_tensor("KaT%d" % i, [128, SEQ], BF16) for i in range(2)]
    NW = 8
    wring = nc.alloc_sbuf_tensor("wring", [128, NW, 1024], BF16)
    wsem = [newsem("w") for _ in range(NW)]
    stat = nc.alloc_sbuf_tensor("stat", [128, 64], F32)
    junk = nc.alloc_sbuf_tensor("junk", [128, 1024], BF16)
    psb = [nc.alloc_psum_tensor("psb%d" % i, [128, 512], F32) for i in range(8)]

    arena = Arena(nc, "arena", 128 * 1024)

    wcnt = [0]

    def wload(src, nel):
        i = wcnt[0] % NW
        wcnt[0] += 1
        dst = wring[:, i, 0:nel]
        S.op("sp", lambda e, dst=dst, src=src: e.dma_start(out=dst, in_=src), writes=[("wr", i)], dma_sem=wsem[i])
        return dst, ("wr", i)

    m0 = arena.off
    NPS = 6
    stf = [arena.alloc([2048], F32) for _ in range(NPS)]
    stb = [arena.alloc([2048], BF16) for _ in range(NPS)]
    sem_in = [newsem("pin") for _ in range(NPS)]
    sem_out = [newsem("pout") for _ in range(NPS)]
    sem_c = newsem("const")
    pcnt = [0]

    def cast_op(eng, dst, src, reads, writes):
        if eng == "act":
            S.op("act", lambda e: e.activation(out=dst, in_=src, func=AF.Copy), reads=reads, writes=writes)
        else:
            S.op(eng, lambda e: e.tensor_copy(out=dst, in_=src), reads=reads, writes=writes)

    def prep_block(src, n, dst=None, dst_sb=None):
        k = pcnt[0] % NPS
        eng = ("dve", "act", "dve", "act", "pool")[pcnt[0] % 5]
        pcnt[0] += 1
        S.op("sp", lambda e: e.dma_start(out=stf[k][:, 0:n], in_=src), writes=[("stf", k)], dma_sem=sem_in[k])
        if dst_sb is not None:
            cast_op(eng, dst_sb, stf[k][:, 0:n], [("stf", k)], [("constb",)])
            return
        cast_op(eng, stb[k][:, 0:n], stf[k][:, 0:n], [("stf", k)], [("stb", k)])
        srcv = stb[k][:, 0:n]
        if len(dst.shape) == 3:
            srcv = srcv.rearrange("p (g j) -> p g j", j=128)
        S.op("sp", lambda e: e.dma_start(out=dst, in_=srcv), reads=[("stb", k)], dma_sem=sem_out[k])

    S.op("sp", lambda e: e.dma_start(out=constf[:, :], in_=cf[:, C_PB:C_PB + 256]), writes=[("constf",)], dma_sem=sem_c)
    S.op("sp", lambda e: e.dma_start(out=gv[:, :], in_=gvec), writes=[("gv",)], dma_sem=sem_c)
    S.op("sp", lambda e: e.dma_start(out=gB[:, :, :], in_=bass.AP(tensor=gbc.tensor, offset=0, ap=[[0, 128], [1024, 2], [1, 1024]])), writes=[("gB",)], dma_sem=sem_c)
    for n0 in range(0, NB16, 2048):
        n = min(2048, NB16 - n0)
        prep_block(cf[:, C_B16 + n0:C_B16 + n0 + n], n, dst_sb=constb[:, n0:n0 + n])
    S.op("sp", lambda e: e.dma_start(out=stf[0][64:72, 0:SEQ], in_=erow), reads=[("constb",)], writes=[("stf", 0)], dma_sem=sem_in[0])
    for i in range(2):
        S.op("dve", lambda e, i=i: e.tensor_copy(out=KaT[i][64:72, :], in_=stf[0][64:72, 0:SEQ]),
             reads=[("stf", 0)], writes=[("KaE", i)])

    def blocks_lhsT(src, K, n_lo, n_hi, dstL, bs):
        for c in range(K // 128):
            for n0 in range(n_lo, n_hi, bs):
                n = min(bs, n_hi - n0)
                g0 = n0 // 128
                d = dstL[g0:g0 + n // 128, :, c, :].rearrange("g p j -> p g j")
                yield (src[c * 128:(c + 1) * 128, n0:n0 + n], n, d)

    def blocks_rhs(src, K, N, dstR):
        for c in range(K // 128):
            yield (src[c * 128:(c + 1) * 128, 0:N], N, dstR[c, :, :])

    for (src_, n_, d_) in blocks_lhsT(w_in, 1024, 0, 3072, WinL, 2048):
        prep_block(src_, n_, dst=d_)
    bg_blocks = []
    bg_blocks += list(blocks_lhsT(w_in, 1024, 3072, 5120, WinL, 1024))
    bg_blocks += list(blocks_lhsT(w_upm, 512, 0, 1024, WupmL, 1024))
    bg_blocks += list(blocks_lhsT(w_ups, 512, 0, 1024, WupsL, 1024))
    bg_blocks += list(blocks_rhs(w_out, 1024, 1024, WoutR))
    bg_blocks += list(blocks_lhsT(w_mi, 1024, 0, 4096, WmiL, 1024))
    bg_blocks += list(blocks_rhs(w_mo, 4096, 1024, WmoR))
    S.barrier()
    arena.off = m0
    if upto == 'prep':
        S.emit(final_waits=[])
        return nc

    mA = arena.off
    hnT = arena.alloc([8, SEQ], BF16)
    tab = arena.alloc([2, SEQ], F32)
    QaT = [arena.alloc([SEQ], BF16) for _ in range(2)]
    QbT = arena.alloc([SEQ], BF16)
    KbT = arena.alloc([SEQ], BF16)
    Vp = arena.alloc([16, 256], BF16)
    Vp4 = Vp.rearrange("p t (h c) -> p t h c", h=2)
    xt = [arena.alloc([1024], F32) for _ in range(2)]
    xn = [arena.alloc([1024], BF16) for _ in range(2)]
    cq = [arena.alloc([512], BF16) for _ in range(3)]
    sq = [arena.alloc([512], BF16) for _ in range(3)]
    E1 = [arena.alloc([512], F32) for _ in range(2)]
    NL = 5
    Lp = [arena.alloc([512], BF16) for _ in range(NL)]
    AT = [arena.alloc([512], BF16) for _ in range(3)]
    Rb = [arena.alloc([512], BF16) for _ in range(3)]
    recb = [arena.alloc([512], F32) for _ in range(2)]
    gm = arena.alloc([16, 8], F32)
    mx = arena.alloc([16, 8], F32)
    thr = arena.alloc([16], F32)
    selm = arena.alloc([16, 8], F32)
    MbPs = [arena.alloc([16, 72], BF16) for _ in range(2)]
    kms = arena.alloc([16], F32)
    kmT = arena.alloc([16], BF16)
    endA = arena.off
    arena.off = mA
    xch = arena.alloc([4, 1024], F32)
    hnTc = arena.alloc([8, 512], BF16)
    xn2 = [arena.alloc([1024], BF16) for _ in range(2)]
    gT = arena.alloc([16, 512], BF16)
    mixT = arena.alloc([8, 512], BF16)
    t1 = [arena.alloc([512], F32) for _ in range(2)]
    t2 = [arena.alloc([512], F32) for _ in range(2)]
    tmpn = [arena.alloc([1024], F32) for _ in range(2)]
    u2T = arena.alloc([32, 512], BF16)
    rl = [arena.alloc([512], BF16) for _ in range(3)]
    ost = [arena.alloc([1024], F32) for _ in range(2)]
    endC = arena.off
    arena.off = max(endA, endC)
    bstf = [arena.alloc([1024], F32) for _ in range(2)]
    bstb = [arena.alloc([1024], BF16) for _ in range(2)]
    bsem_in = [newsem("bin") for _ in range(2)]
    bsem_out = [newsem("bout") for _ in range(2)]
    bgc = [0]

    def bg_step(eng):
        if not bg_blocks:
            return
        src, n, dst = bg_blocks.pop(0)
        k = bgc[0] % 2
        bgc[0] += 1
        S.op("sp", lambda e: e.dma_start(out=bstf[k][:, 0:n], in_=src), writes=[("bstf", k)], dma_sem=bsem_in[k])
        cast_op(eng, bstb[k][:, 0:n], bstf[k][:, 0:n], [("bstf", k)], [("bstb", k)])
        srcv = bstb[k][:, 0:n]
        if len(dst.shape) == 3:
            srcv = srcv.rearrange("p (g j) -> p g j", j=128)
        S.op("sp", lambda e: e.dma_start(out=dst, in_=srcv), reads=[("bstb", k)], dma_sem=bsem_out[k])

    xsem = [newsem("x") for _ in range(4)]
    osem = [newsem("o") for _ in range(2)]
    tsem = newsem("tab")
    dsem = newsem("dbg")
    stc = [0]

    gpre = gv[:, 0:8]
    gpre2 = gv[:, 8:16]
    bgate = gv[:, 16:32]

    def ps_bf(b):
        return psb[b][:, :].bitcast(BF16).rearrange("p (c t) -> p c t", c=8)

    tb = [0]

    def norm_transpose(src, src_keys, dst, dst_keys, g, xnbuf, xnkey, banks, split=False):
        k = stc[0] % 16
        stc[0] += 1
        c0 = 4 * k
        sk = ("st", k)
        S.op("act", lambda e: e.activation(out=junk[:, :], in_=src, func=AF.Square, accum_out=stat[:, c0:c0 + 1]),
             reads=src_keys, writes=[sk])
        S.op("act", lambda e: e.activation(out=stat[:, c0 + 1:c0 + 2], in_=stat[:, c0:c0 + 1], func=AF.Sqrt,
                                           scale=1.0 / DM, bias=EPS), reads=[sk], writes=[sk])
        S.op("dve", lambda e: e.reciprocal(out=stat[:, c0 + 2:c0 + 3], in_=stat[:, c0 + 1:c0 + 2]), reads=[sk], writes=[sk])
        j = tb[0] % 2
        tb[0] += 1
        S.op("dve", lambda e: e.tensor_scalar(out=xnbuf[j], in0=src, scalar1=stat[:, c0 + 2:c0 + 3], scalar2=None,
                                              op0=ALU.mult), reads=src_keys + [sk], writes=[(xnkey, j)])
        bank = banks[j]
        pv = ps_bf(bank)

        def part2():
            def tr(e):
                for dc in range(8):
                    ins = e.transpose(out=pv[:, dc, :], in_=xnbuf[j][:, dc * 128:(dc + 1) * 128], identity=ident)
                return ins
            S.op("pe", tr, reads=[(xnkey, j)], writes=[("ps", bank)])
            S.op("dve", lambda e: e.tensor_tensor(out=dst, in0=pv, in1=g.unsqueeze(2).to_broadcast([128, 8, 128]), op=ALU.mult),
                 reads=[("ps", bank)], writes=dst_keys)
        if split:
            return part2
        part2()

    def proj_fm(wt, wkey, rhs_fn, rkeys, bank):
        def fn(e):
            for dc in range(8):
                ins = e.matmul(psb[bank][:, :], lhsT=wt[:, dc * 128:(dc + 1) * 128], rhs=rhs_fn(dc), start=(dc == 0), stop=(dc == 7))
            return ins
        S.op("pe", fn, reads=[wkey] + rkeys, writes=[("ps", bank)])

    def hn_chunk(tc):
        return (lambda dc: hnT[:, dc, tc * 512:(tc + 1) * 512]), [("hnT", 4 * tc + i) for i in range(4)]

    final_ops = []
    dbg_ops = []

    def dbg_dump(name, src_ap, keys, dst_ap=None):
        if dbg and name in dbg_out:
            d = dbg_out[name] if dst_ap is None else dst_ap
            o = S.op("sp", lambda e: e.dma_start(out=d, in_=src_ap), reads=keys, dma_sem=dsem)
            dbg_ops.append(o)

    def seq_body(s):
        S.op("sp", lambda e: e.dma_start(out=tab[:, :, :], in_=cf[:, 0:4096].rearrange("p (a b) -> p a b", a=2)),
             writes=[("tab",)], dma_sem=tsem)
        S.op("pool", lambda e: e.memset(Vp[:, :, :], 1.0), writes=[("Vp", t) for t in range(16)])
        for hl_ in range(2):
            S.op("pool", lambda e, hl_=hl_: e.memset(MbPs[hl_][:, :, :], 0.0), writes=[("MbP", hl_)])
        for tt in range(16):
            k = tt % 2
            S.op("sp", lambda e, k=k, tt=tt: e.dma_start(out=xt[k], in_=x[s, tt * 128:(tt + 1) * 128, :]),
                 writes=[("xt", k)], dma_sem=xsem[k])
            norm_transpose(xt[k], [("xt", k)], hnT[:, :, tt * 128:(tt + 1) * 128], [("hnT", tt)], gpre, xn, "xn", (0, 1))
        if s == 0:
            dbg_dump("hnT", hnT[:, :, :], [("hnT", t) for t in range(16)])
        if upto == 'A':
            raise StopBuild()

        def moba_pair(hp):
            stages = []
            for which in ("k", "q"):
                g = (4 + hp) if which == "k" else hp
                wt, wkey = wload(WinL[g].rearrange("p c j -> p (c j)"), 1024)
                for tc in range(4):
                    stages.append((which, tc, wt, wkey))

            PB3 = (0, 1, 6)

            def st_proj(i):
                which, tc, wt, wkey = stages[i]
                rf, rk = hn_chunk(tc)
                proj_fm(wt, wkey, rf, rk, PB3[i % 3])

            def st_rest(i):
                which, tc, wt, wkey = stages[i]
                pb = i % 2
                pi = i % 3
                pbk = PB3[pi]
                S.op("dve", lambda e: e.tensor_tensor(out=cq[pi], in0=psb[pbk][:, :], in1=tab[:, 0, tc * 512:(tc + 1) * 512], op=ALU.mult),
                     reads=[("ps", pbk), ("tab",)], writes=[("cq", pi)])
                S.op("dve", lambda e: e.tensor_tensor(out=sq[pi], in0=psb[pbk][:, :], in1=tab[:, 1, tc * 512:(tc + 1) * 512], op=ALU.mult),
                     reads=[("ps", pbk), ("tab",)], writes=[("sq", pi)])
                for hl in range(2):
                    rb = 2 + 2 * hl + pb
                    I_, R_ = (IA, RA) if hl == 0 else (IB, RB)

                    def rope(e, rb=rb, I_=I_, R_=R_):
                        e.matmul(psb[rb][0:64, :], lhsT=I_, rhs=cq[pi], start=True, stop=False)
                        return e.matmul(psb[rb][0:64, :], lhsT=R_, rhs=sq[pi], start=False, stop=True)
                    S.op("pe", rope, reads=[("cq", pi), ("sq", pi)], writes=[("ps", rb)])
                    if which == "k":
                        def kev(e, rb=rb, hl=hl):
                            for b2 in range(2):
                                ins = e.activation(out=KaT[hl][0:64, tc * 512 + b2 * 256:tc * 512 + (b2 + 1) * 256],
                                                   in_=psb[rb][0:64, b2 * 256:(b2 + 1) * 256], func=AF.Copy,
                                                   accum_out=kms[0:64, hl * 8 + 2 * tc + b2:hl * 8 + 2 * tc + b2 + 1])
                            return ins
                        S.op("act", kev, reads=[("ps", rb)], writes=[("KaT", hl, tc), ("kms", hl, tc)])
                    else:
                        S.op("act", lambda e, rb=rb, hl=hl: e.activation(out=QaT[hl][0:64, tc * 512:(tc + 1) * 512], in_=psb[rb][0:64, :], func=AF.Copy, scale=0.125),
                             reads=[("ps", rb)], writes=[("QaT", hl, tc)])
                if which == "k" and tc == 3:
                    S.op("dve", lambda e: e.tensor_copy(out=kmT[0:64, :], in_=kms[0:64, :]),
                         reads=[("kms", hl, t) for hl in range(2) for t in range(4)], writes=[("kmT",)])

            st_proj(0)
            st_proj(1)
            for i in range(8):
                if i + 2 < 8:
                    st_proj(i + 2)
                st_rest(i)
            if upto == 'kq':
                raise StopBuild()
            wt, wkey = wload(WinL[8 + hp].rearrange("p c j -> p (c j)"), 1024)
            for tg in range(4):
                pb = tg % 2

                def vproj(e, tg=tg, pb=pb, wt=wt):
                    for i in range(4):
                        tt = 4 * tg + i
                        for dc in range(8):
                            ins = e.matmul(psb[pb][:, i * 128:(i + 1) * 128], lhsT=hnT[:, dc, tt * 128:(tt + 1) * 128],
                                           rhs=wt[:, dc * 128:(dc + 1) * 128], start=(dc == 0), stop=(dc == 7))
                    return ins
                S.op("pe", vproj, reads=[wkey] + [("hnT", 4 * tg + i) for i in range(4)], writes=[("ps", pb)])
                S.op("act", lambda e, tg=tg, pb=pb: e.activation(out=Vp4[:, 4 * tg:4 * tg + 4, :, 0:64], in_=psb[pb][:, :].rearrange("p (t h c) -> p t h c", t=4, h=2), func=AF.Copy),
                     reads=[("ps", pb)], writes=[("Vp", 4 * tg + i) for i in range(4)])
            if upto == 'v':
                raise StopBuild()
            for hl in (range(1) if upto == 'gate0' else range(2)):
                def gate(e, hl=hl):
                    for qt in range(16):
                        ins = e.matmul(psb[6][:, qt * 8:(qt + 1) * 8], lhsT=QaT[hl][0:64, qt * 128:(qt + 1) * 128],
                                       rhs=kmT[0:64, hl * 8:(hl + 1) * 8], start=True, stop=True)
                    return ins
                S.op("pe", gate, reads=[("QaT", hl, tc) for tc in range(4)] + [("kmT",)], writes=[("ps", 6)])
                S.op("dve", lambda e, hl=hl: e.tensor_tensor(out=gm, in0=psb[6][:, 0:128].rearrange("p (a b) -> p a b", a=16), in1=PB, op=ALU.add),
                     reads=[("ps", 6)], writes=[("gm",)])

                def top8(e):
                    for qt in range(16):
                        ins = e.max(out=mx[:, qt, :], in_=gm[:, qt, :])
                    return ins
                S.op("dve", top8, reads=[("gm",)], writes=[("mx",)])
                S.op("dve", lambda e: e.tensor_scalar(out=thr, in0=mx[:, :, 2], scalar1=-1e29, scalar2=None, op0=ALU.max),
                     reads=[("mx",)], writes=[("thr",)])
                S.op("dve", lambda e: e.tensor_tensor(out=selm, in0=gm, in1=thr.unsqueeze(2).to_broadcast([128, 16, 8]), op=ALU.is_ge),
                     reads=[("gm",), ("thr",)], writes=[("selm",)])
                S.op("dve", lambda e, hl=hl: e.tensor_tensor(out=MbPs[0][:, :, 64:72], in0=selm, in1=C1, op=ALU.add),
                     reads=[("selm",)], writes=[("MbP", 0)])
                for tc in range(4):
                    def mtr(e, tc=tc, hl=hl):
                        for i in range(4):
                            qt = 4 * tc + i
                            ins = e.matmul(psb[7][0:72, i * 128:(i + 1) * 128], lhsT=MbPs[0][:, qt, :], rhs=ident, start=True, stop=True)
                        return ins
                    S.op("pe", mtr, reads=[("MbP", 0)], writes=[("ps", 7)])
                    S.op("dve", lambda e, tc=tc, hl=hl: e.tensor_copy(out=QaT[hl][64:72, tc * 512:(tc + 1) * 512], in_=psb[7][64:72, :]),
                         reads=[("ps", 7)], writes=[("QaM", hl, tc)])
            if s == 0 and hp == 0:
                dbg_dump("QaT0", QaT[0][:, :], [("QaT", 0, tc) for tc in range(4)] + [("QaM", 0, tc) for tc in range(4)])
                dbg_dump("KaT0", KaT[0][:, :], [("KaT", 0, tc) for tc in range(4)] + [("KaE", 0)])
            if upto in ('gate', 'gate0'):
                raise StopBuild()
            tiles = []
            chain = 0
            for c in range(4):
                for hl in range(2):
                    nk = 4 * c + 4
                    for kt in range(nk):
                        tiles.append((c, hl, kt, nk, chain))
                    chain += 1
            N = len(tiles)

            def rng(c, kt):
                j = kt - 4 * c
                return (128 * j if j > 0 else 0), j

            def m_qk(i):
                c, hl, kt, nk, ch = tiles[i]
                sbk = i % 3
                lo, j = rng(c, kt)

                def fn(e):
                    ins = e.matmul(psb[sbk][:, lo:512], lhsT=KaT[hl][0:72, kt * 128:(kt + 1) * 128], rhs=QaT[hl][0:72, c * 512 + lo:(c + 1) * 512],
                                   start=True, stop=(j < 0))
                    if j >= 0:
                        ins = e.matmul(psb[sbk][:, lo:lo + 128], lhsT=ident, rhs=NMM[j][:, lo:lo + 128], start=False, stop=True)
                    return ins
                S.op("pe", fn, reads=[("KaT", hl, kt // 4), ("KaE", hl), ("QaT", hl, c), ("QaM", hl, c)], writes=[("ps", sbk)])

            def m_exp(i):
                c, hl, kt, nk, ch = tiles[i]
                sbk = i % 3
                r = i % 3
                lo, j = rng(c, kt)
                S.op("act", lambda e: e.activation(out=AT[r][:, lo:512], in_=psb[sbk][:, lo:512], func=AF.Exp), reads=[("ps", sbk)], writes=[("AT", r)])

            def m_pv(i):
                c, hl, kt, nk, ch = tiles[i]
                r = i % 3
                ob = 4 + ch % 2
                lo, j = rng(c, kt)
                S.op("pe", lambda e: e.matmul(psb[ob][:, lo:512], lhsT=Vp4[:, kt, hl, :], rhs=AT[r][:, lo:512], start=(kt == 0), stop=(kt == nk - 1)),
                     reads=[("AT", r), ("Vp", kt)], writes=[("ps", ob)])
                if kt == nk - 1:
                    rr = ch % 2
                    po = 64 * hl
                    S.op("dve", lambda e: e.reciprocal(out=recb[rr][64:128, :], in_=psb[ob][64:128, :]), reads=[("ps", ob)], writes=[("rec", rr)])
                    S.op("dve", lambda e: e.tensor_tensor(out=yTm[po:po + 64, hp, c * 512:(c + 1) * 512], in0=psb[ob][0:64, :],
                                                          in1=recb[rr][64:128, :], op=ALU.mult),
                         reads=[("ps", ob), ("rec", rr)], writes=[("yTm", hp, hl, c)])

            m_qk(0)
            for st_ in range(N + 1):
                if s == 0 and st_ % 6 == 3:
                    bg_step("pool")
                if st_ + 1 < N:
                    m_qk(st_ + 1)
                if st_ < N:
                    m_exp(st_)
                if 0 <= st_ - 1 < N:
                    m_pv(st_ - 1)
        for hp_ in range(4):
            moba_pair(hp_)
            if upto == 'moba1':
                break
        if s == 0:
            dbg_dump("yTm", yTm[:, :, :], [("yTm", hp, hl, c) for hp in range(4) for hl in range(2) for c in range(4)])

        def sb_pair(hp):
            for which in ("q", "k"):
                g = (12 + hp) if which == "q" else (16 + hp)
                wt, wkey = wload(WinL[g].rearrange("p c j -> p (c j)"), 1024)
                dstT = QbT if which == "q" else KbT
                for tc in range(4):
                    pb = 6 + tc % 2
                    rf, rk = hn_chunk(tc)
                    proj_fm(wt, wkey, rf, rk, pb)
                    sc = 0.125 if which == "q" else 1.0
                    S.op("dve", lambda e, pb=pb, tc=tc, dstT=dstT, sc=sc: e.tensor_scalar(out=dstT[:, tc * 512:(tc + 1) * 512], in0=psb[pb][:, :], scalar1=sc, scalar2=None, op0=ALU.mult),
                         reads=[("ps", pb)], writes=[(which + "bT", tc)])
            wt, wkey = wload(WinL[20 + hp].rearrange("p c j -> p (c j)"), 1024)
            for tg in range(4):
                pb = 6 + tg % 2

                def vproj2(e, tg=tg, pb=pb, wt=wt):
                    for i in range(4):
                        tt = 4 * tg + i
                        for dc in range(8):
                            ins = e.matmul(psb[pb][:, i * 128:(i + 1) * 128], lhsT=hnT[:, dc, tt * 128:(tt + 1) * 128],
                                           rhs=wt[:, dc * 128:(dc + 1) * 128], start=(dc == 0), stop=(dc == 7))
                    return ins
                S.op("pe", vproj2, reads=[wkey] + [("hnT", 4 * tg + i) for i in range(4)], writes=[("ps", pb)])
                S.op("dve", lambda e, tg=tg, pb=pb: e.tensor_copy(out=Vp4[:, 4 * tg:4 * tg + 4, :, 0:64], in_=psb[pb][:, :].rearrange("p (t h c) -> p t h c", t=4, h=2)),
                     reads=[("ps", pb)], writes=[("Vp", 4 * tg + i) for i in range(4)])
            tiles = []
            chain = 0
            for c in range(4):
                for hl in range(2):
                    nk = 4 * c + 4
                    for n in range(nk):
                        tiles.append((c, hl, nk - 1 - n, n, nk, chain))
                    chain += 1
            N = len(tiles)
            rsrc = {}

            def s_qk(i):
                c, hl, kt, n, nk, ch = tiles[i]
                zb = i % 4
                po = 64 * hl
                diag = kt >= 4 * c

                def fn(e):
                    ins = e.matmul(psb[zb][:, :], lhsT=KbT[po:po + 64, kt * 128:(kt + 1) * 128], rhs=QbT[po:po + 64, c * 512:(c + 1) * 512],
                                   start=True, stop=not diag)
                    if diag:
                        ins = e.matmul(psb[zb][:, :], lhsT=ident, rhs=NMS[kt - 4 * c], start=False, stop=True)
                    return ins
                S.op("pe", fn, reads=[("kbT", kt // 4), ("qbT", c)], writes=[("ps", zb)])

            def s_e1(i):
                zb = i % 4
                r = i % 2
                S.op("act", lambda e: e.activation(out=E1[r], in_=psb[zb][:, :], func=AF.Exp), reads=[("ps", zb)], writes=[("E1", r)])

            def s_lp(i):
                r = i % 2
                l = i % NL
                S.op("act", lambda e: e.activation(out=Lp[l], in_=E1[r], func=AF.Ln, bias=1.0), reads=[("E1", r)], writes=[("Lp", l)])

            def s_rupd(i):
                c, hl, kt, n, nk, ch = tiles[i]
                if n + 1 >= nk:
                    return
                if n == 0:
                    rsrc[i + 1] = (Lp[i % NL], ("Lp", i % NL))
                    return
                prev, pkey = rsrc[i]
                rn = i % 3
                S.op("pool", lambda e: e.tensor_tensor(out=Rb[rn], in0=prev, in1=Lp[i % NL], op=ALU.add),
                     reads=[pkey, ("Lp", i % NL)], writes=[("R", rn)])
                rsrc[i + 1] = (Rb[rn], ("R", rn))

            def s_trir(i):
                c, hl, kt, n, nk, ch = tiles[i]
                zb = i % 4
                l = i % NL
                reads = [("Lp", l)]
                rs = None
                if n >= 1:
                    rs, rkey = rsrc[i]
                    reads.append(rkey)

                def fn(e):
                    ins = e.matmul(psb[zb][:, :], lhsT=NTI, rhs=Lp[l], start=False, stop=(n == 0), skip_group_check=True)
                    if n >= 1:
                        ins = e.matmul(psb[zb][:, :], lhsT=NEG1, rhs=rs, start=False, stop=True, skip_group_check=True)
                    return ins
                S.op("pe", fn, reads=reads, writes=[("ps", zb)])

            def s_at(i):
                zb = i % 4
                r = i % 3
                S.op("act", lambda e: e.activation(out=AT[r], in_=psb[zb][:, :], func=AF.Exp), reads=[("ps", zb)], writes=[("AT", r)])

            def s_av(i):
                c, hl, kt, n, nk, ch = tiles[i]
                r = i % 3
                ob = 4 + ch % 2
                S.op("pe", lambda e: e.matmul(psb[ob][:, :], lhsT=Vp4[:, kt, hl, :], rhs=AT[r], start=(n == 0), stop=(n == nk - 1)),
                     reads=[("AT", r), ("Vp", kt)], writes=[("ps", ob)])
                if n == nk - 1:
                    po = 64 * hl
                    S.op("dve", lambda e: e.tensor_copy(out=yTs[po:po + 64, hp, c * 512:(c + 1) * 512], in_=psb[ob][0:64, :]),
                         reads=[("ps", ob)], writes=[("yTs", hp, hl, c)])

            s_qk(0)
            for st_ in range(N + 3):
                if s == 0 and st_ % 6 == 3:
                    bg_step("dve")
                if st_ + 1 < N:
                    s_qk(st_ + 1)
                if st_ < N:
                    s_e1(st_)
                if 0 <= st_ - 2 < N:
                    s_at(st_ - 2)
                if st_ < N:
                    s_lp(st_)
                    s_rupd(st_)
                if 0 <= st_ - 1 < N:
                    s_trir(st_ - 1)
                if 0 <= st_ - 3 < N:
                    s_av(st_ - 3)
        if upto in ('moba', 'moba1'):
            dbg_dump("yTm", yTm[:, :, :], [])
            raise StopBuild()
        for hp_ in range(4):
            sb_pair(hp_)
            if upto == 'sb1':
                break
        if s == 0:
            dbg_dump("yTs", yTs[:, :, :], [("yTs", hp, hl, c) for hp in range(4) for hl in range(2) for c in range(4)])

        if upto in ('sb', 'sb1'):
            raise StopBuild()
        while s == 0 and bg_blocks:
            bg_step("dve")
        S.barrier()
        hk = [("hnTc", i) for i in range(4)]

        def c_head(tc, tiles_):
            p2 = []
            for i in tiles_:
                tok0 = tc * 512 + i * 128
                S.op("sp", lambda e, i=i, tok0=tok0: e.dma_start(out=xch[:, i, :], in_=x[s, tok0:tok0 + 128, :]),
                     writes=[("xch", i)], dma_sem=xsem[i])
                p2.append(norm_transpose(xch[:, i, :], [("xch", i)], hnTc[:, :, i * 128:(i + 1) * 128], [("hnTc", i)], gpre, xn2, "xn2", (0, 1), split=True))
            return p2

        def c_gates(half):
            hkh = [("hnTc", 2 * half), ("hnTc", 2 * half + 1)]
            for j in range(16):
                wt, wkey = wload(WinL[24 + j].rearrange("p c j -> p (c j)"), 1024)
                pb = 2 + j % 2

                def fn(e, wt=wt, pb=pb):
                    for dc in range(8):
                        ins = e.matmul(psb[pb][:, 0:256], lhsT=wt[:, dc * 128:(dc + 1) * 128], rhs=hnTc[:, dc, half * 256:(half + 1) * 256],
                                       start=(dc == 0), stop=(dc == 7))
                    return ins
                S.op("pe", fn, reads=[wkey] + hkh, writes=[("ps", pb)])
                S.op("act", lambda e, j=j, pb=pb: e.activation(out=gT[:, j, half * 256:(half + 1) * 256], in_=psb[pb][:, 0:256], func=AF.Sigmoid, bias=bgate[:, j:j + 1]),
                     reads=[("ps", pb), ("gv",)], writes=[("gT", j, half)])

        def c_upmix(tc):
            for j in range(8):
                wm, wmk = wload(WupmL[j].rearrange("p c j -> p (c j)"), 512)
                ws, wsk = wload(WupsL[j].rearrange("p c j -> p (c j)"), 512)
                bm = 4 + j % 2
                bs = 6 + j % 2
                r = j % 2

                def up(e, wm=wm, ws=ws, bm=bm, bs=bs):
                    for p in range(4):
                        e.matmul(psb[bm][:, :], lhsT=wm[:, p * 128:(p + 1) * 128], rhs=yTm[:, p, tc * 512:(tc + 1) * 512], start=(p == 0), stop=(p == 3))
                    for p in range(4):
                        ins = e.matmul(psb[bs][:, :], lhsT=ws[:, p * 128:(p + 1) * 128], rhs=yTs[:, p, tc * 512:(tc + 1) * 512], start=(p == 0), stop=(p == 3))
                    return ins
                S.op("pe", up, reads=[wmk, wsk], writes=[("ps", bm), ("ps", bs)])
                S.op("dve", lambda e, j=j, bm=bm, r=r: e.tensor_tensor(out=t1[r], in0=psb[bm][:, :], in1=gT[:, j, :], op=ALU.mult),
                     reads=[("ps", bm), ("gT", j, 0), ("gT", j, 1)], writes=[("t1", r)])
                S.op("dve", lambda e, j=j, bs=bs, r=r: e.tensor_tensor(out=t2[r], in0=psb[bs][:, :], in1=gT[:, 8 + j, :], op=ALU.mult),
                     reads=[("ps", bs), ("gT", 8 + j, 0), ("gT", 8 + j, 1)], writes=[("t2", r)])
                S.op("pool", lambda e, j=j, r=r: e.tensor_tensor(out=mixT[:, j, :], in0=t1[r], in1=t2[r], op=ALU.add),
                     reads=[("t1", r), ("t2", r)], writes=[("mixT", j)])

        def c_oproj(half):
            for j in range(8):
                wo, wok = wload(WoutR[j], 1024)

                def oproj(e, j=j, wo=wo):
                    for i in (2 * half, 2 * half + 1):
                        for h in range(2):
                            ins = e.matmul(psb[2 * i + h][:, :], lhsT=mixT[:, j, i * 128:(i + 1) * 128], rhs=wo[:, h * 512:(h + 1) * 512],
                                           start=(j == 0), stop=(j == 7))
                    return ins
                S.op("pe", oproj, reads=[wok, ("mixT", j)], writes=[("ps", b) for b in range(4 * half, 4 * half + 4)])

        def post_norm(i, gsel, dst, dst_keys, res_keys):
            k = stc[0] % 16
            stc[0] += 1
            c0 = 4 * k
            sk = ("st", k)
            b0, b1 = 2 * i, 2 * i + 1

            def sqs(e):
                e.activation(out=junk[:, 0:512], in_=psb[b0][:, :], func=AF.Square, accum_out=stat[:, c0:c0 + 1])
                return e.activation(out=junk[:, 512:1024], in_=psb[b1][:, :], func=AF.Square, accum_out=stat[:, c0 + 1:c0 + 2])
            S.op("act", sqs, reads=[("ps", b0), ("ps", b1)], writes=[sk])
            S.op("dve", lambda e: e.tensor_tensor(out=stat[:, c0 + 2:c0 + 3], in0=stat[:, c0:c0 + 1], in1=stat[:, c0 + 1:c0 + 2], op=ALU.add),
                 reads=[sk], writes=[sk])
            S.op("act", lambda e: e.activation(out=stat[:, c0 + 3:c0 + 4], in_=stat[:, c0 + 2:c0 + 3], func=AF.Sqrt, scale=1.0 / DM, bias=EPS),
                 reads=[sk], writes=[sk])
            S.op("dve", lambda e: e.reciprocal(out=stat[:, c0:c0 + 1], in_=stat[:, c0 + 3:c0 + 4]), reads=[sk], writes=[sk])
            r = i % 2

            def nrm(e):
                e.scalar_tensor_tensor(out=tmpn[r][:, 0:512], in0=psb[b0][:, :], scalar=stat[:, c0:c0 + 1], in1=gB[:, gsel, 0:512],
                                       op0=ALU.mult, op1=ALU.mult)
                return e.scalar_tensor_tensor(out=tmpn[r][:, 512:1024], in0=psb[b1][:, :], scalar=stat[:, c0:c0 + 1], in1=gB[:, gsel, 512:1024],
                                              op0=ALU.mult, op1=ALU.mult)
            S.op("dve", nrm, reads=[("ps", b0), ("ps", b1), sk, ("gB",)], writes=[("tmpn", r)])
            S.op("pool", lambda e: e.tensor_tensor(out=dst, in0=xch[:, i, :], in1=tmpn[r], op=ALU.add),
                 reads=[("tmpn", r)] + res_keys, writes=dst_keys)

        def c_postmid(i):
            post_norm(i, 0, xch[:, i, :], [("xch", i)], [("xch", i)])

        def c_dnorm(i):
            return norm_transpose(xch[:, i, :], [("xch", i)], hnTc[:, :, i * 128:(i + 1) * 128], [("hnTc", i)], gpre2, xn2, "xn2", (0, 1), split=True)

        def c_mlpin(half):
            hkh = [("hnTc", 2 * half), ("hnTc", 2 * half + 1)]
            for f in range(32):
                wt, wkey = wload(WmiL[f].rearrange("p c j -> p (c j)"), 1024)
                pb = (f % 4) if half == 0 else (2 + f % 4)

                def fn(e, wt=wt, pb=pb):
                    for dc in range(8):
                        ins = e.matmul(psb[pb][:, 0:256], lhsT=wt[:, dc * 128:(dc + 1) * 128], rhs=hnTc[:, dc, half * 256:(half + 1) * 256],
                                       start=(dc == 0), stop=(dc == 7))
                    return ins
                S.op("pe", fn, reads=[wkey] + hkh, writes=[("ps", pb)])
                r = f % 3
                S.op("act", lambda e, pb=pb, r=r: e.activation(out=rl[r][:, 0:256], in_=psb[pb][:, 0:256], func=AF.Relu), reads=[("ps", pb)], writes=[("rl", r)])
                S.op("pool" if f % 2 == 0 else "dve", lambda e, f=f, r=r: e.tensor_tensor(out=u2T[:, f, half * 256:(half + 1) * 256], in0=rl[r][:, 0:256], in1=rl[r][:, 0:256], op=ALU.mult),
                     reads=[("rl", r)], writes=[("u2T", f, half)])

        def c_mproj(half):
            for f in range(32):
                wo, wok = wload(WmoR[f], 1024)

                def mproj(e, f=f, wo=wo):
                    for i in (2 * half, 2 * half + 1):
                        for h in range(2):
                            ins = e.matmul(psb[2 * i + h][:, :], lhsT=u2T[:, f, i * 128:(i + 1) * 128], rhs=wo[:, h * 512:(h + 1) * 512],
                                           start=(f == 0), stop=(f == 31))
                    return ins
                S.op("pe", mproj, reads=[wok, ("u2T", f, half)], writes=[("ps", b) for b in range(4 * half, 4 * half + 4)])

        def c_postfin(tc, i):
            r = i % 2
            post_norm(i, 1, ost[r], [("ost", r)], [("xch", i)])
            tok0 = tc * 512 + i * 128
            o = S.op("sp", lambda e, r=r, tok0=tok0: e.dma_start(out=out[s, tok0:tok0 + 128, :], in_=ost[r]),
                     reads=[("ost", r)], dma_sem=osem[r])
            final_ops.append(o)

        for p2 in c_head(0, (0, 1)):
            p2()
        nx = c_head(0, (2, 3))
        for tc in range(4):
            c_gates(0)
            for p2 in nx:
                p2()
            c_gates(1)
            c_upmix(tc)
            c_oproj(0)
            c_oproj(1)
            c_postmid(0)
            c_postmid(1)
            for p2 in [c_dnorm(0), c_dnorm(1)]:
                p2()
            c_postmid(2)
            c_postmid(3)
            nx = [c_dnorm(2), c_dnorm(3)]
            c_mlpin(0)
            for p2 in nx:
                p2()
            c_mlpin(1)
            c_mproj(0)
            c_mproj(1)
            c_postfin(tc, 0)
            c_postfin(tc, 1)
            if tc < 3:
                for p2 in c_head(tc + 1, (0, 1)):
                    p2()
            c_postfin(tc, 2)
            c_postfin(tc, 3)
            if tc < 3:
                nx = c_head(tc + 1, (2, 3))
        S.barrier()

    try:
        for s_ in range(nseq):
            seq_body(s_)
    except StopBuild:
        S.barrier()
    S.emit(final_waits=final_ops[-2:] + dbg_ops)
    return nc


def _consts():
    cfa = np.zeros((128, NCF), np.float32)
    half = 32
    inv = (np.float32(10000.0) ** (-(np.arange(half, dtype=np.float32) * np.float32(2.0)) / np.float32(64))).astype(np.float32)
    ang = (np.arange(SEQ, dtype=np.float32)[:, None] * inv[None, :]).astype(np.float32)
    ang = np.concatenate([ang, ang], axis=-1)
    cos = np.cos(ang).astype(np.float32)
    sin = np.sin(ang).astype(np.float32)
    p = np.arange(128)
    cfa[:, C_COS:C_COS + SEQ] = cos[:, p % 64].T
    cfa[:, C_SIN:C_SIN + SEQ] = sin[:, p % 64].T
    qt = np.arange(16)[:, None]
    b = np.arange(8)[None, :]
    own = qt // 2
    pbm = np.where(b < own, 0.0, -1e30).astype(np.float32)
    c1m = (b == own).astype(np.float32) - 1.0
    cfa[:, C_PB:C_PB + 128] = pbm.reshape(1, 128)
    cfa[:, C_C1:C_C1 + 128] = c1m.reshape(1, 128)
    B = np.zeros((128, NB16), np.float32)
    B[:, B_ID:B_ID + 128] = np.eye(128)
    for m in range(64):
        for base, col in ((0, B_RA), (64, B_RB)):
            if m < 32:
                B[base + m + 32, col + m] = -1.0
            else:
                B[base + m - 32, col + m] = 1.0
    jj = np.arange(128)[:, None]
    kk = np.arange(128)[None, :]
    B[:, B_NTI:B_NTI + 128] = np.where(jj >= kk, -1.0, 0.0)
    B[:, B_NEG1:B_NEG1 + 128] = -1.0
    k = np.arange(128)[:, None]
    q = np.arange(512)[None, :]
    for j in range(4):
        i = q // 128
        qq = q % 128
        inv_s = (i < j) | ((i == j) & (k >= qq))
        inv_m = (i < j) | ((i == j) & (k > qq))
        B[:, B_NMS + 512 * j:B_NMS + 512 * (j + 1)] = np.where(inv_s, -BIG, 0.0)
        B[:, B_NMM + 512 * j:B_NMM + 512 * (j + 1)] = np.where(inv_m, -BIG, 0.0)
    cfa[:, C_B16:] = B
    er = np.zeros((8, SEQ), np.float32)
    for bb in range(8):
        er[bb, bb * 256:(bb + 1) * 256] = BIG
    return cfa, er


_NC_CACHE = {}


def kernel(x, g_pre_mix, w_in, b_gate, w_up_moba, w_up_sb, w_out, g_post_mix,
           g_pre_mlp, w_mlp_in, w_mlp_out, g_post_mlp):
    f = lambda a: np.ascontiguousarray(np.asarray(a, dtype=np.float32))
    x = f(x)
    nseq = x.shape[0] // NCORES
    if nseq not in _NC_CACHE:
        _NC_CACHE[nseq] = build(nseq)
    nc = _NC_CACHE[nseq]
    cfa, er = _consts()
    gvec = np.concatenate([f(g_pre_mix)[0].reshape(8, 128).T, f(g_pre_mlp)[0].reshape(8, 128).T,
                           f(b_gate)[0].reshape(16, 128).T], axis=1)
    gbc = np.stack([f(g_post_mix)[0], f(g_post_mlp)[0]], axis=0)
    shared = {
        "w_in": f(w_in)[0], "w_up_moba": f(w_up_moba)[0], "w_up_sb": f(w_up_sb)[0], "w_out": f(w_out)[0],
        "w_mlp_in": f(w_mlp_in)[0], "w_mlp_out": f(w_mlp_out)[0],
        "gvec": np.ascontiguousarray(gvec), "gbc": np.ascontiguousarray(gbc), "cf": cfa, "erow": er,
    }
    in_maps = []
    for c in range(NCORES):
        m = dict(shared)
        m["x"] = np.ascontiguousarray(x[c * nseq:(c + 1) * nseq])
        in_maps.append(m)
    res = run_bass_kernel_spmd(nc, in_maps, core_ids=list(range(NCORES)))
    return np.concatenate([r["out"] for r in res.results], axis=0)
```
